# Optimizing a Trainium2 kernel written in Bass

```python
import jax, jax.numpy as jnp
from jax import lax
import numpy as np

D_MODEL = 1024
BATCH = 4
SEQ = 4096
DEPTH = 2
DEC_BATCH = 128
DEC_SEQ = 4
PAST_LEN = 16384
PAGE_SIZE = 128

HEAD_DIM = 64
SWA_Q_HEADS = 8
SWA_KV_HEADS = 2
SWA_GROUP = SWA_Q_HEADS // SWA_KV_HEADS
WINDOW = 128
ROPE_THETA = 10000.0
GLA_HEADS = 4
GLA_DK = 64
GLA_DV = 128
GLA_LOWRANK = 16
GLA_GATE_TEMP = 16.0
GLA_CHUNK = 64
MEM_LEN = 256
XA_HEADS = 4
XA_HEAD_DIM = 128
XA_W = XA_HEADS * XA_HEAD_DIM
D_FF = 2816
EPS = 1e-6

SWA_Q_W = SWA_Q_HEADS * HEAD_DIM
SWA_KV_W = SWA_KV_HEADS * HEAD_DIM
GLA_K_W = GLA_HEADS * GLA_DK
GLA_V_W = GLA_HEADS * GLA_DV
MIX_W = SWA_Q_W + GLA_V_W
_IN_WIDTHS = (SWA_Q_W, SWA_KV_W, SWA_KV_W, GLA_K_W, GLA_K_W, GLA_V_W, GLA_V_W, GLA_LOWRANK)
IN_W = sum(_IN_WIDTHS)
IN_SPLITS = tuple(int(s) for s in np.cumsum(_IN_WIDTHS)[:-1])

kernel_name = "hymba_swa_gla_macaron_memxattn_step"


def rms_norm(x, g):
    xf = x.astype(jnp.float32)
    y = xf * lax.rsqrt(jnp.mean(xf * xf, axis=-1, keepdims=True) + EPS)
    return (y * g.astype(jnp.float32)).astype(x.dtype)


def rope(x, pos):
    half = x.shape[-1] // 2
    inv = ROPE_THETA ** (-jnp.arange(half, dtype=jnp.float32) / half)
    ang = pos.astype(jnp.float32)[:, None] * inv[None, :]
    cos = jnp.cos(ang)[None, :, None, :]
    sin = jnp.sin(ang)[None, :, None, :]
    xf = x.astype(jnp.float32)
    x1, x2 = xf[..., :half], xf[..., half:]
    return jnp.concatenate([x1 * cos - x2 * sin, x2 * cos + x1 * sin], axis=-1).astype(x.dtype)


def swiglu(x, wg, wu, wd):
    return (jax.nn.silu(x @ wg) * (x @ wu)) @ wd


def _swa_core(q, kc, vc, sinks, valid):
    s = jnp.einsum('bnqhgd,bnkhd->bnhgqk', q, kc, preferred_element_type=jnp.float32) * HEAD_DIM ** -0.5
    s = jnp.where(valid[None, :, None, None], s, -jnp.inf)
    sink = sinks.astype(jnp.float32).reshape(1, 1, SWA_KV_HEADS, SWA_GROUP, 1, 1)
    m = jnp.maximum(jnp.max(s, axis=-1, keepdims=True), sink)
    e = jnp.exp(s - m)
    pr = e / (jnp.sum(e, axis=-1, keepdims=True) + jnp.exp(sink - m))
    return jnp.einsum('bnhgqk,bnkhd->bnqhgd', pr.astype(vc.dtype), vc)


def swa_prompt(q, k, v, sinks):
    B, T, Hq, D = q.shape
    nb = T // WINDOW
    qb = q.reshape(B, nb, WINDOW, SWA_KV_HEADS, SWA_GROUP, D)

    def band(z):
        zb = z.reshape(B, nb, WINDOW, SWA_KV_HEADS, D)
        prev = jnp.concatenate([jnp.zeros_like(zb[:, :1]), zb[:, :-1]], axis=1)
        return jnp.concatenate([prev, zb], axis=2)

    i = jnp.arange(WINDOW)[:, None]
    j = jnp.arange(2 * WINDOW)[None, :]
    rel = WINDOW + i - j
    n = jnp.arange(nb)[:, None, None]
    valid = ((rel >= 0) & (rel < WINDOW))[None] & ((n > 0) | (j[None] >= WINDOW))
    o = _swa_core(qb, band(k), band(v), sinks, valid)
    return o.reshape(B, T, Hq * D)


def swa_decode(q, kcat, vcat, sinks):
    B, T, Hq, D = q.shape
    Tk = kcat.shape[1]
    i = jnp.arange(T)[:, None]
    j = jnp.arange(Tk)[None, :]
    rel = WINDOW + i - j
    valid = ((rel >= 0) & (rel < WINDOW))[None]
    o = _swa_core(q.reshape(B, 1, T, SWA_KV_HEADS, SWA_GROUP, D), kcat[:, None], vcat[:, None], sinks, valid)
    return o.reshape(B, T, Hq * D)


def gla_chunked(q, k, v, log_a, s0):
    B, T, H, DK = q.shape
    C = min(GLA_CHUNK, T)
    n = T // C

    def to_chunks(z):
        return z.reshape(B, n, C, H, z.shape[-1]).transpose(1, 0, 3, 2, 4)

    causal = jnp.tril(jnp.ones((C, C), dtype=bool))

    def step(S, inp):
        qc, kc, vc, gc = inp
        b = jnp.cumsum(gc, axis=-2)
        b_last = b[..., -1:, :]
        q_t = qc * jnp.exp(b)
        k_t = kc * jnp.exp(-b)
        A = jnp.where(causal, jnp.einsum('bhqd,bhkd->bhqk', q_t, k_t), 0.0)
        o = jnp.einsum('bhqd,bhdv->bhqv', q_t, S) + jnp.einsum('bhqk,bhkv->bhqv', A, vc)
        k_dec = kc * jnp.exp(b_last - b)
        S_new = jnp.exp(b_last)[..., 0, :, None] * S + jnp.einsum('bhkd,bhkv->bhdv', k_dec, vc)
        return S_new, o

    S, o = lax.scan(step, s0, (to_chunks(q), to_chunks(k), to_chunks(v), to_chunks(log_a)))
    o = o.transpose(1, 0, 3, 2, 4).reshape(B, T, H, v.shape[-1])
    return o, S


def hybrid_mix(h, pos, p, swa_cache, gla_state):
    B, T, _ = h.shape
    z = h @ p['w_in']
    q_s, k_s, v_s, q_g, k_g, v_g, g_g, lr = jnp.split(z, IN_SPLITS, axis=-1)
    q_s = rope(rms_norm(q_s.reshape(B, T, SWA_Q_HEADS, HEAD_DIM), p['swa_q_norm']), pos)
    k_s = rope(rms_norm(k_s.reshape(B, T, SWA_KV_HEADS, HEAD_DIM), p['swa_k_norm']), pos)
    v_s = v_s.reshape(B, T, SWA_KV_HEADS, HEAD_DIM)
    if swa_cache is None:
        a_out = swa_prompt(q_s, k_s, v_s, p['swa_sinks'])
        new_k, new_v = k_s[:, -WINDOW:], v_s[:, -WINDOW:]
    else:
        kcat = jnp.concatenate([swa_cache[0].astype(k_s.dtype), k_s], axis=1)
        vcat = jnp.concatenate([swa_cache[1].astype(v_s.dtype), v_s], axis=1)
        a_out = swa_decode(q_s, kcat, vcat, p['swa_sinks'])
        new_k, new_v = kcat[:, -WINDOW:], vcat[:, -WINDOW:]
    qf = q_g.reshape(B, T, GLA_HEADS, GLA_DK).astype(jnp.float32) * GLA_DK ** -0.5
    kf = k_g.reshape(B, T, GLA_HEADS, GLA_DK).astype(jnp.float32)
    vf = v_g.reshape(B, T, GLA_HEADS, GLA_DV).astype(jnp.float32)
    log_a = jax.nn.log_sigmoid((lr @ p['gla_w_gate'] + p['gla_b_gate']).astype(jnp.float32)) / GLA_GATE_TEMP
    log_a = log_a.reshape(B, T, GLA_HEADS, GLA_DK)
    if gla_state is None:
        s0 = jnp.zeros((B, GLA_HEADS, GLA_DK, GLA_DV), jnp.float32)
    else:
        s0 = gla_state.astype(jnp.float32)
    o, S = gla_chunked(qf, kf, vf, log_a, s0)
    o = rms_norm(o, p['gla_out_norm']).astype(h.dtype) * jax.nn.silu(g_g.reshape(B, T, GLA_HEADS, GLA_DV))
    mix = jnp.concatenate([a_out, o.reshape(B, T, GLA_V_W)], axis=-1) @ p['w_out']
    return mix, new_k, new_v, S.astype(h.dtype)


def mem_kv(mem, p):
    Bm, M, _ = mem.shape
    m = rms_norm(mem, p['mem_norm'])
    k = rms_norm((m @ p['xa_wk']).reshape(Bm, M, XA_HEADS, XA_HEAD_DIM), p['xa_k_norm'])
    v = (m @ p['xa_wv']).reshape(Bm, M, XA_HEADS, XA_HEAD_DIM)
    return k, v


def cross_attn(h, mk, mv, p):
    B, T, _ = h.shape
    q = rms_norm((h @ p['xa_wq']).reshape(B, T, XA_HEADS, XA_HEAD_DIM), p['xa_q_norm'])
    s = jnp.einsum('bqhd,bkhd->bhqk', q, mk.astype(q.dtype), preferred_element_type=jnp.float32) * XA_HEAD_DIM ** -0.5
    a = jax.nn.softmax(s, axis=-1).astype(h.dtype)
    o = jnp.einsum('bhqk,bkhd->bqhd', a, mv.astype(h.dtype)).reshape(B, T, XA_W)
    return o @ p['xa_wo']


def decoder_layer(x, pos, p, swa_cache, gla_state, mk, mv):
    x = x + 0.5 * swiglu(rms_norm(x, p['ffn1_norm']), p['ffn1_wg'], p['ffn1_wu'], p['ffn1_wd'])
    mix, nk, nv, S = hybrid_mix(rms_norm(x, p['mix_norm']), pos, p, swa_cache, gla_state)
    x = x + mix
    x = x + cross_attn(rms_norm(x, p['xa_norm']), mk, mv, p)
    x = x + 0.5 * swiglu(rms_norm(x, p['ffn2_norm']), p['ffn2_wg'], p['ffn2_wu'], p['ffn2_wd'])
    return x, nk, nv, S


def setup_inputs(seed: int = 0) -> dict:
    key = jax.random.key(seed)
    ks = iter(jax.random.split(key, 64))
    L = DEPTH

    def nrm(shape, scale=1.0):
        return jax.random.normal(next(ks), shape, jnp.float32) * scale

    def w(shape, fan_in):
        return nrm(shape, fan_in ** -0.5)

    def gain(shape):
        return 1.0 + nrm(shape, 0.02)

    return {
        'x_prompt': nrm((BATCH, SEQ, D_MODEL)),
        'x_sample': nrm((DEC_BATCH, DEC_SEQ, D_MODEL)),
        'cache_swa_k': nrm((L, DEC_BATCH, WINDOW, SWA_KV_HEADS, HEAD_DIM)),
        'cache_swa_v': nrm((L, DEC_BATCH, WINDOW, SWA_KV_HEADS, HEAD_DIM)),
        'state_gla': nrm((L, DEC_BATCH, GLA_HEADS, GLA_DK, GLA_DV)),
        'cache_mem_k': nrm((L, DEC_BATCH, MEM_LEN, XA_HEADS, XA_HEAD_DIM)),
        'cache_mem_v': nrm((L, DEC_BATCH, MEM_LEN, XA_HEADS, XA_HEAD_DIM)),
        'mem_prompt': nrm((BATCH, MEM_LEN, D_MODEL)),
        'ffn1_norm': gain((L, D_MODEL)),
        'ffn1_wg': w((L, D_MODEL, D_FF), D_MODEL),
        'ffn1_wu': w((L, D_MODEL, D_FF), D_MODEL),
        'ffn1_wd': w((L, D_FF, D_MODEL), D_FF),
        'mix_norm': gain((L, D_MODEL)),
        'w_in': w((L, D_MODEL, IN_W), D_MODEL),
        'swa_q_norm': gain((L, HEAD_DIM)),
        'swa_k_norm': gain((L, HEAD_DIM)),
        'swa_sinks': nrm((L, SWA_Q_HEADS), 0.5),
        'gla_w_gate': w((L, GLA_LOWRANK, GLA_K_W), GLA_LOWRANK),
        'gla_b_gate': nrm((L, GLA_K_W), 0.1),
        'gla_out_norm': gain((L, GLA_DV)),
        'w_out': w((L, MIX_W, D_MODEL), MIX_W),
        'xa_norm': gain((L, D_MODEL)),
        'mem_norm': gain((L, D_MODEL)),
        'xa_wq': w((L, D_MODEL, XA_W), D_MODEL),
        'xa_wk': w((L, D_MODEL, XA_W), D_MODEL),
        'xa_wv': w((L, D_MODEL, XA_W), D_MODEL),
        'xa_q_norm': gain((L, XA_HEAD_DIM)),
        'xa_k_norm': gain((L, XA_HEAD_DIM)),
        'xa_wo': w((L, XA_W, D_MODEL), XA_W),
        'ffn2_norm': gain((L, D_MODEL)),
        'ffn2_wg': w((L, D_MODEL, D_FF), D_MODEL),
        'ffn2_wu': w((L, D_MODEL, D_FF), D_MODEL),
        'ffn2_wd': w((L, D_FF, D_MODEL), D_FF),
    }


def reference(x_prompt, x_sample, cache_swa_k, cache_swa_v, state_gla, cache_mem_k, cache_mem_v,
              mem_prompt, ffn1_norm, ffn1_wg, ffn1_wu, ffn1_wd, mix_norm, w_in, swa_q_norm,
              swa_k_norm, swa_sinks, gla_w_gate, gla_b_gate, gla_out_norm, w_out, xa_norm, mem_norm,
              xa_wq, xa_wk, xa_wv, xa_q_norm, xa_k_norm, xa_wo, ffn2_norm, ffn2_wg, ffn2_wu, ffn2_wd):
    pos_p = jnp.arange(x_prompt.shape[1])
    pos_s = PAST_LEN + jnp.arange(x_sample.shape[1])
    yp, ys = x_prompt, x_sample
    kp_l, vp_l, sp_l, mkp_l, mvp_l, ks_l, vs_l, ss_l = [], [], [], [], [], [], [], []
    for l in range(DEPTH):
        p = {name: arr[l] for name, arr in (
            ('ffn1_norm', ffn1_norm), ('ffn1_wg', ffn1_wg), ('ffn1_wu', ffn1_wu), ('ffn1_wd', ffn1_wd),
            ('mix_norm', mix_norm), ('w_in', w_in), ('swa_q_norm', swa_q_norm), ('swa_k_norm', swa_k_norm),
            ('swa_sinks', swa_sinks), ('gla_w_gate', gla_w_gate), ('gla_b_gate', gla_b_gate),
            ('gla_out_norm', gla_out_norm), ('w_out', w_out), ('xa_norm', xa_norm), ('mem_norm', mem_norm),
            ('xa_wq', xa_wq), ('xa_wk', xa_wk), ('xa_wv', xa_wv), ('xa_q_norm', xa_q_norm),
            ('xa_k_norm', xa_k_norm), ('xa_wo', xa_wo), ('ffn2_norm', ffn2_norm), ('ffn2_wg', ffn2_wg),
            ('ffn2_wu', ffn2_wu), ('ffn2_wd', ffn2_wd))}
        mk_p, mv_p = mem_kv(mem_prompt, p)
        yp, kp, vp, sp = decoder_layer(yp, pos_p, p, None, None, mk_p, mv_p)
        ys, kss, vss, sss = decoder_layer(ys, pos_s, p, (cache_swa_k[l], cache_swa_v[l]), state_gla[l],
                                         cache_mem_k[l], cache_mem_v[l])
        kp_l.append(kp); vp_l.append(vp); sp_l.append(sp); mkp_l.append(mk_p); mvp_l.append(mv_p)
        ks_l.append(kss); vs_l.append(vss); ss_l.append(sss)
    return (yp, ys, jnp.stack(kp_l), jnp.stack(vp_l), jnp.stack(sp_l), jnp.stack(mkp_l), jnp.stack(mvp_l),
            jnp.stack(ks_l), jnp.stack(vs_l), jnp.stack(ss_l))
```

```python
import os
import numpy as np
from contextlib import ExitStack
import concourse.bass as bass
import concourse.mybir as mybir
from concourse.bass_utils import run_bass_kernel_spmd

F32 = mybir.dt.float32
BF16 = mybir.dt.bfloat16
ALU = mybir.AluOpType
AF = mybir.ActivationFunctionType
ENGS = ("pe", "dve", "act", "pool", "sp")

D = 1024
DFF = 2816
NJ = 22
SEQ = 4096
NSEQ_S = 16
NS_TOK = 64
EPS = 1e-6
NCH = 171
TS = 1024
NSLOT = 6


class Op:
    __slots__ = ("idx", "eng", "fn", "deps", "dma_key", "dma_cnt", "needed", "cnt", "waits")

    def __init__(self, idx, eng, fn):
        self.idx = idx
        self.eng = eng
        self.fn = fn
        self.deps = set()
        self.dma_key = None
        self.dma_cnt = 0
        self.needed = False
        self.cnt = 0
        self.waits = []


class Prog:
    def __init__(self, nc):
        self.nc = nc
        self.ops = []
        self.last_w = {}
        self.readers = {}
        self.dma_counts = {}
        self.es = ExitStack()

    def sbuf(self, name, shape, dtype):
        return self.es.enter_context(self.nc.sbuf_tensor(name, list(shape), dtype))

    def psum(self, name, shape, dtype=F32):
        return self.es.enter_context(self.nc.psum_tensor(name, list(shape), dtype))

    def _rec(self, eng, fn, reads, writes):
        ex = [k for k in reads if isinstance(k, str) and k[0] == "B" and k[1:].isdigit()]
        if ex:
            reads = [k for k in reads if k not in ex]
            writes = list(writes) + [k for k in ex if k not in writes]
        op = Op(len(self.ops), eng, fn)
        for k in reads:
            w = self.last_w.get(k)
            if w is not None:
                op.deps.add(w)
        for k in writes:
            w = self.last_w.get(k)
            if w is not None:
                op.deps.add(w)
            for r in self.readers.get(k, ()):
                op.deps.add(r)
        for k in reads:
            self.readers.setdefault(k, []).append(op.idx)
        for k in writes:
            self.last_w[k] = op.idx
            self.readers[k] = []
        op.deps.discard(op.idx)
        self.ops.append(op)
        return op

    def op(self, eng, fn, reads=(), writes=()):
        return self._rec(eng, fn, reads, writes)

    def dma(self, eng, fn, reads=(), writes=(), key=None):
        op = self._rec(eng, fn, reads, writes)
        if key is None:
            key = writes[0]
        op.dma_key = key
        self.dma_counts[key] = self.dma_counts.get(key, 0) + 1
        op.dma_cnt = self.dma_counts[key]
        return op

    def emit(self, final_prefix="out"):
        nc = self.nc
        ops = self.ops
        seen = {e: {} for e in ENGS}
        seen_dma = {e: {} for e in ENGS}
        for op in ops:
            per_eng = {}
            for d in op.deps:
                p = ops[d]
                if p.dma_key is not None:
                    cur = seen_dma[op.eng].get(p.dma_key, 0)
                    if p.dma_cnt > cur:
                        seen_dma[op.eng][p.dma_key] = p.dma_cnt
                        op.waits.append(("dma", p.dma_key, p.dma_cnt))
                    continue
                if p.eng == "pe" and op.eng == "pe":
                    continue
                if d > per_eng.get(p.eng, -1):
                    per_eng[p.eng] = d
            for pe_, d in per_eng.items():
                if d > seen[op.eng].get(pe_, -1):
                    seen[op.eng][pe_] = d
                    ops[d].needed = True
                    op.waits.append(("eng", pe_, d))
        cnts = {e: 0 for e in ENGS}
        for op in ops:
            if op.dma_key is None and op.needed:
                cnts[op.eng] += 1
                op.cnt = cnts[op.eng]
        sems = {e: self.es.enter_context(nc.semaphore("s_" + e)) for e in ENGS}
        dsems = {}
        for k in self.dma_counts:
            dsems[k] = self.es.enter_context(nc.semaphore("d%d" % len(dsems)))
        self.n_sems = len(sems) + len(dsems)
        by_eng = {e: [o for o in ops if o.eng == e] for e in ENGS}
        fin = [(dsems[k], 16 * self.dma_counts[k]) for k in self.dma_counts if str(k).startswith(final_prefix)]

        def run(ename, e):
            for op in by_eng[ename]:
                for w in op.waits:
                    if w[0] == "dma":
                        e.wait_ge(dsems[w[1]], 16 * w[2])
                    else:
                        e.wait_ge(sems[w[1]], ops[w[2]].cnt)
                ins = op.fn(e)
                if op.dma_key is not None:
                    ins.then_inc(dsems[op.dma_key], 16)
                elif op.needed:
                    ins.then_inc(sems[ename], 1)
            if ename == "sp":
                for s, v in fin:
                    e.wait_ge(s, v)

        with nc.Block() as block:
            @block.tensor
            def _(e):
                run("pe", e)

            @block.vector
            def _(e):
                run("dve", e)

            @block.scalar
            def _(e):
                run("act", e)

            @block.gpsimd
            def _(e):
                run("pool", e)

            @block.sync
            def _(e):
                run("sp", e)
        self.es.close()


def _colchunk(W, cols):
    cols = np.asarray(list(cols))
    out = np.zeros((128, 8, 128), np.float32)
    sub = W[:, cols]
    out[:, :, :len(cols)] = sub.reshape(8, 128, len(cols)).transpose(1, 0, 2)
    return out.reshape(128, 1024)


def _ffn_chunks(wg, wu, wd):
    ch = []
    for j in range(NJ):
        ch.append(_colchunk(wg, range(j * 128, (j + 1) * 128)))
        ch.append(_colchunk(wu, range(j * 128, (j + 1) * 128)))
    wd3 = np.zeros((24 * 128, D), np.float32)
    wd3[:DFF] = wd
    wdr = wd3.reshape(24, 128, 8, 128)
    for dc in range(8):
        for pc in range(3):
            ch.append(np.ascontiguousarray(wdr[pc * 8:(pc + 1) * 8, :, dc, :].transpose(1, 0, 2)).reshape(128, 1024))
    return ch


def _win_groups():
    g = []
    g.append(list(range(512, 640)))
    g.append(list(range(640, 768)))
    for c in range(4):
        g.append(list(range(c * 64, c * 64 + 64)) + list(range((c + 4) * 64, (c + 4) * 64 + 64)))
    g.append(list(range(2304, 2320)))
    for hc in range(2):
        g.append(list(range(768 + hc * 128, 768 + (hc + 1) * 128)))
    for hc in range(2):
        g.append(list(range(1024 + hc * 128, 1024 + (hc + 1) * 128)))
    for h in range(4):
        g.append(list(range(1280 + h * 128, 1280 + (h + 1) * 128)))
    for h in range(4):
        g.append(list(range(1792 + h * 128, 1792 + (h + 1) * 128)))
    return g


def _mixrow(m, p):
    if m < 4:
        head = m if p < 64 else m + 4
        return head * 64 + (p % 64)
    return 512 + (m - 4) * 128 + p


def _layer_stream(inp, l):
    ch = _ffn_chunks(inp["ffn1_wg"][l], inp["ffn1_wu"][l], inp["ffn1_wd"][l])
    win = inp["w_in"][l]
    for cols in _win_groups():
        ch.append(_colchunk(win, cols))
    wout = inp["w_out"][l]
    rows = np.array([[_mixrow(m, p) for m in range(8)] for p in range(128)])
    for dc in range(8):
        ch.append(np.ascontiguousarray(wout[rows][:, :, dc * 128:(dc + 1) * 128]).reshape(128, 1024))
    wq = inp["xa_wq"][l]
    for h in range(4):
        ch.append(_colchunk(wq, range(h * 128, (h + 1) * 128)))
    wo = inp["xa_wo"][l].reshape(4, 128, 8, 128)
    for dcp in range(4):
        ch.append(np.ascontiguousarray(wo[:, :, 2 * dcp:2 * dcp + 2, :].transpose(1, 2, 0, 3)).reshape(128, 1024))
    ch += _ffn_chunks(inp["ffn2_wg"][l], inp["ffn2_wu"][l], inp["ffn2_wd"][l])
    assert len(ch) == NCH
    return np.stack(ch)


C_ID, C_ONES, C_BD64, C_ROT, C_TRI, C_TREV, C_GM, C_TRIS, C_TREVS, C_GMS = [i * 128 for i in range(10)]
C_OWN = 1280
C_PREV = C_OWN + 512
C_SC = C_PREV + 512
C_SN = C_SC + 256
C_BM = C_SN + 256
NCONST = C_BM + 16

PL_G = 0
PL_QN, PL_KN, PL_ON, PL_XQ = 40, 41, 42, 43
PL_SINK = 44
PL_XK = 48
PL_WG = 176
NPL = 176 + 256


def _consts():
    c = np.zeros((128, NCONST), np.float32)
    p = np.arange(128)
    c[:, C_ID:C_ID + 128] = np.eye(128)
    c[:, C_ONES:C_ONES + 128] = 1.0
    c[:, C_BD64:C_BD64 + 128] = (p[:, None] // 64 == p[None, :] // 64)
    rt = np.zeros((128, 128), np.float32)
    for m in range(128):
        if m % 64 < 32:
            rt[m + 32, m] = -1.0
        else:
            rt[m - 32, m] = 1.0
    c[:, C_ROT:C_ROT + 128] = rt
    same = (p[:, None] // 64 == p[None, :] // 64)
    c[:, C_TRI:C_TRI + 128] = (p[:, None] <= p[None, :])
    c[:, C_TREV:C_TREV + 128] = same & (p[:, None] > p[None, :])
    q = np.arange(64)
    c[:, C_GM:C_GM + 64] = ((p[:, None] % 64) <= q[None, :])
    s64 = np.arange(64)
    sm = (s64[:, None] // 4 == s64[None, :] // 4)
    c[:64, C_TRIS:C_TRIS + 64] = sm & (s64[:, None] <= s64[None, :])
    c[:64, C_TREVS:C_TREVS + 64] = sm & (s64[:, None] > s64[None, :])
    c[:64, C_GMS:C_GMS + 64] = sm & (s64[:, None] <= s64[None, :])
    own = (p[:, None] <= p[None, :]).astype(np.float32)
    prev = (p[:, None] > p[None, :]).astype(np.float32)
    c[:, C_OWN:C_OWN + 512] = np.tile(own, (1, 4))
    c[:, C_PREV:C_PREV + 512] = np.tile(prev, (1, 4))
    tq = (s64 % 4)
    sc = (p[:, None] >= (tq[None, :] + 1)).astype(np.float32)
    c[:, C_SC:C_SC + 256] = np.tile(sc, (1, 4))
    sn = (sm & (s64[:, None] <= s64[None, :])).astype(np.float32)
    c[:64, C_SN:C_SN + 256] = np.tile(sn, (1, 4))
    c[:64, C_BM:C_BM + 16] = (s64[:, None] // 4 == np.arange(16)[None, :])
    return c


def _rope_tables():
    half = 32
    inv = (10000.0 ** (-np.arange(half, dtype=np.float32) / half)).astype(np.float32)
    pos = np.concatenate([np.arange(SEQ), np.tile(16384 + np.arange(4), NSEQ_S)]).astype(np.float32)
    ang = pos[None, :] * inv[(np.arange(128) % 64) % 32][:, None]
    ang = ang.astype(np.float32)
    return np.stack([np.cos(ang), np.sin(ang)]).astype(np.float32)


def _layer_params(inp, l):
    a = np.zeros((128, NPL), np.float32)
    for i, nm in enumerate(["ffn1_norm", "mix_norm", "xa_norm", "ffn2_norm", "mem_norm"]):
        a[:, PL_G + i * 8:PL_G + (i + 1) * 8] = inp[nm][l].reshape(8, 128).T
    p = np.arange(128)
    a[:, PL_QN] = inp["swa_q_norm"][l][p % 64]
    a[:, PL_KN] = inp["swa_k_norm"][l][p % 64]
    a[:, PL_ON] = inp["gla_out_norm"][l]
    a[:, PL_XQ] = inp["xa_q_norm"][l]
    a[:, PL_SINK:PL_SINK + 4] = inp["swa_sinks"][l].reshape(2, 4)[p // 64]
    a[:, PL_XK:PL_XK + 128] = inp["xa_k_norm"][l][None, :]
    a[0:16, PL_WG:PL_WG + 256] = inp["gla_w_gate"][l]
    a[16, PL_WG:PL_WG + 256] = inp["gla_b_gate"][l]
    return a


def build(n_st=4, stage=99, sts=None):
    nc = bass.Bass("TRN2", target_bir_lowering=False)

    def din(name, shape):
        return nc.dram_tensor(name, list(shape), F32, kind="ExternalInput").ap()

    def dout(name, shape):
        return nc.dram_tensor(name, list(shape), F32, kind="ExternalOutput").ap()

    _specs = {
        "xp": [SEQ, D], "xs": [NS_TOK, D], "ws": [2, NCH, 128, 1024], "wpre": [2, 8, 128, 1024],
        "consts": [128, NCONST], "lpar": [2, 128, NPL], "rope": [2, 128, SEQ + NS_TOK], "mem": [256, D],
        "c_swak": [2, NSEQ_S, 128, 128], "c_swav": [2, NSEQ_S, 128, 128], "c_gla": [2, NSEQ_S, 4, 64, 128],
        "c_memk": [2, NSEQ_S, 256, 512], "c_memv": [2, NSEQ_S, 256, 512],
    }
    _ospecs = {
        "yp": [SEQ, D], "ys": [NS_TOK, D], "o_swak": [2, 128, 128], "o_swav": [2, 128, 128],
        "o_gla": [2, 4, 64, 128], "o_memk": [2, 256, 512], "o_memv": [2, 256, 512],
        "o_swaks": [2, NSEQ_S, 128, 128], "o_swavs": [2, NSEQ_S, 128, 128], "o_glas": [2, NSEQ_S, 4, 64, 128],
    }
    _decl = {}

    def T(name):
        if name not in _decl:
            if name in _specs:
                _decl[name] = din(name, _specs[name])
            else:
                _decl[name] = dout(name, _ospecs[name])
        return _decl[name]

    P = Prog(nc)
    TMAX = TS + NS_TOK
    x = P.sbuf("x", [128, 8, TMAX], F32)
    xn = P.sbuf("xn", [128, 8, TMAX], BF16)
    hb = P.sbuf("hb", [128, NJ, TMAX], BF16)
    slots = [P.sbuf("wsl%d" % i, [128, 1024], BF16) for i in range(NSLOT)]
    sg = [P.sbuf("sg%d" % i, [128, 512], F32) for i in range(4)]
    sq = [P.sbuf("sq%d" % i, [128, 512], BF16) for i in range(2)]
    rstd = P.sbuf("rstd", [128, 512], F32)
    rstd2 = P.sbuf("rstd2", [128, 512], F32)
    chain = {"n": 0}

    def scratch():
        chain["n"] += 1
        if chain["n"] % 2:
            return rstd, "rstd", sg[0], "sg0", sg[1], "sg1"
        return rstd2, "rstd2", sg[2], "sg2", sg[3], "sg3"

    stg = [P.sbuf("stg%d" % i, [128, 1024], F32) for i in range(2)]
    cst = P.sbuf("cst", [128, NCONST], F32)
    lp = P.sbuf("lp", [128, 2, NPL], F32)
    onesb = P.sbuf("onesb", [128, 128], BF16)
    B = [P.psum("B%d" % i, [128, 512]) for i in range(8)]

    def MM(out, lhsT, rhs, start, stop, r, w):
        P.op("pe", lambda e: e.matmul(out, lhsT=lhsT, rhs=rhs, start=start, stop=stop), reads=r, writes=w)

    def TR(out, in_, r, w):
        P.op("pe", lambda e: e.transpose(out=out, in_=in_, identity=cst[0:in_.shape[0], C_ID:C_ID + in_.shape[0]]),
             reads=list(r) + ["cst"], writes=w)

    def ACT(out, in_, func, r, w, **kw):
        P.op("act", lambda e: e.activation(out=out, in_=in_, func=func, **kw), reads=r, writes=w)

    def STT(out, in0, scalar, in1, op0, op1, r, w, eng="dve"):
        P.op(eng, lambda e: e.scalar_tensor_tensor(out=out, in0=in0, scalar=scalar, in1=in1, op0=op0, op1=op1),
             reads=r, writes=w)

    def TT(out, in0, in1, op, r, w, eng="dve"):
        P.op(eng, lambda e: e.tensor_tensor(out=out, in0=in0, in1=in1, op=op), reads=r, writes=w)

    def RECIP(out, in_, r, w):
        P.op("dve", lambda e: e.reciprocal(out=out, in_=in_), reads=r, writes=w)

    def COPY(out, in_, r, w, eng="dve"):
        P.op(eng, lambda e: e.tensor_copy(out=out, in_=in_), reads=r, writes=w)

    def DMA(out, in_, r, w, key=None, eng="sp"):
        P.dma(eng, lambda e: e.dma_start(out=out, in_=in_), reads=r, writes=w, key=key)

    DMA(cst[:], T("consts"), [], ["cst"])
    DMA(lp[:, 0, :], T("lpar")[0], [], ["lp"])
    DMA(lp[:, 1, :], T("lpar")[1], ["lp"], ["lp"])
    COPY(onesb[:], cst[:, C_ONES:C_ONES + 128], ["cst"], ["onesb"])


    TT_ = TMAX
    kTb = P.sbuf("kTb", [128, 2, 128 + TMAX], BF16)
    Vb = P.sbuf("Vb", [128, 2, 10, 128], BF16)
    Sst = P.sbuf("Sst", [128, 2, 2, 128], F32)
    Sbf = P.sbuf("Sbf", [128, 2, 2, 128], BF16)
    kTm = P.sbuf("kTm", [128, 2, 4, 256], BF16)
    Vm = P.sbuf("Vm", [128, 2, 2, 512], BF16)
    rp = P.sbuf("rp", [128, 2, TMAX], F32)
    rot = P.sbuf("rot", [128, 2, 2, 128], F32)
    esk = P.sbuf("esk", [128, 2, 4], F32)
    bd64b = P.sbuf("bd64b", [128, 128], BF16)
    mown = P.sbuf("mown", [128, 512], BF16)
    mprev = P.sbuf("mprev", [128, 512], BF16)
    msc = P.sbuf("msc", [128, 256], BF16)
    msn = P.sbuf("msn", [128, 256], BF16)
    mgs = P.sbuf("mgs", [128, 64], BF16)
    wgb = P.sbuf("wgb", [32, 2, 256], BF16)
    lrT = P.sbuf("lrT", [32, TMAX], BF16)
    pbuf = [P.sbuf("pb%d" % i, [128, 512], BF16) for i in range(2)]
    dn = P.sbuf("dn", [128, 512], F32)
    Asb = P.sbuf("Asb", [128, 4, 128], BF16)
    lsp = P.sbuf("lsp", [128, 256], F32)
    nbs = P.sbuf("nbs", [128, 2, NS_TOK], F32)
    ebl = P.sbuf("ebl", [128, 2, 9 + 16], F32)
    ko = P.sbuf("ko", [128, 128], F32)
    vo = P.sbuf("vo", [128, 128], F32)
    kTs = P.sbuf("kTs", [128, 4, 256], BF16)
    Vs = P.sbuf("Vs", [128, 2, 512], BF16)
    psm = P.sbuf("psm", [128, 32], BF16)
    ss4 = P.sbuf("ss4", [128, 8], F32)
    pnew = P.sbuf("pnew", [64, 2, 256], BF16)
    pcach = P.sbuf("pcach", [128, 2, 256], BF16)
    Kbl = P.sbuf("Kbl", [64, 4, 256], BF16)
    bmb = P.sbuf("bmb", [64, 16], BF16)
    zb = P.sbuf("zb", [128, 128], BF16)
    epst = P.sbuf("epst", [128, 1], F32)
    qxs = P.sbuf("qxs", [128, 4, NS_TOK], BF16)
    qz = P.sbuf("qz", [128, 2, 2, NS_TOK], BF16)
    msx = P.sbuf("msx", [128, NSEQ_S, 64], BF16)
    hbf = hb[:].rearrange("p j t -> p (j t)")
    _cv = {"o": 0}

    def carve(shape, dtype):
        n = int(np.prod(shape[1:]))
        nb = n if dtype == BF16 else 2 * n
        a = _cv["o"]
        _cv["o"] += nb
        assert _cv["o"] <= NJ * TMAX, _cv["o"]
        v = hbf[:, a:a + nb]
        if dtype != BF16:
            v = v.bitcast(F32)
        if len(shape) == 3:
            v = v.rearrange("p (a b) -> p a b", a=shape[1])
        return v

    qT = carve([128, 4, TMAX], BF16)
    qg = carve([128, 2, TMAX], BF16)
    kg = carve([128, 2, TMAX], BF16)
    ktok = carve([128, 9, 256], BF16)
    Vg = carve([128, 9, 512], BF16)
    gate = carve([128, 4, TMAX], BF16)
    S0 = carve([128, 2 * 4, 128], F32)
    S0b = carve([128, 2 * 4, 128], BF16)
    Kst = P.sbuf("Kst", [128, 4, 128], F32)
    KcT = P.sbuf("KcT", [128, 4, 128], BF16)
    Vc = P.sbuf("Vc", [128, 4, 128], BF16)

    def fence(key="HBF"):
        P.op("dve", lambda e: e.memset(ebl[0:1, 0, 8:9], 0.0), reads=[], writes=[key])

    for (dst, c0_, n_) in ((bd64b, C_BD64, 128), (mown, C_OWN, 512), (mprev, C_PREV, 512), (msc, C_SC, 256),
                           (msn, C_SN, 256), (mgs, C_GMS, 64)):
        COPY(dst[0:128, :], cst[:, c0_:c0_ + n_], ["cst"], ["cm%d" % c0_])
    for l in range(2):
        COPY(wgb[:, l, :], lp[0:32, l, PL_WG:PL_WG + 256], ["lp"], ["wgb"])
        for qk, col in ((0, PL_QN), (1, PL_KN)):
            P.op("dve", lambda e, l=l, qk=qk, col=col: e.tensor_scalar(
                out=rot[:, l, qk, :], in0=cst[:, C_ROT:C_ROT + 128], scalar1=lp[:, l, col:col + 1], scalar2=None,
                op0=ALU.mult), reads=["cst", "lp"], writes=["rot"])
        ACT(esk[:, l, :], lp[:, l, PL_SINK:PL_SINK + 4], AF.Exp, ["lp"], ["esk"])
    COPY(bmb[:], cst[0:64, C_BM:C_BM + 16], ["cst"], ["bmb"])
    P.op("dve", lambda e: e.memset(zb[:], 0.0), writes=["zb"])
    P.op("dve", lambda e: e.memset(epst[:], EPS), writes=["epst"])
    P.op("dve", lambda e: e.memset(msx[:], 0.0), writes=["msx"])
    for s_ in range(NSEQ_S):
        P.op("dve", lambda e, s_=s_: e.memset(msx[:, s_, 4 * s_:4 * s_ + 4], 1.0), reads=["msx"], writes=["msx"])
    P.op("dve", lambda e: e.memset(Sst[:], 0.0), writes=["Sst0", "Sst1"])
    P.op("dve", lambda e: e.memset(Sbf[:], 0.0), writes=["Sbf0", "Sbf1"])
    P.op("dve", lambda e: e.memset(lrT[:], 1.0), writes=["lrT"])
    P.op("dve", lambda e: e.memset(Vb[:], 0.0), writes=["Vb0", "Vb1"])
    P.op("dve", lambda e: e.memset(kTb[:], 0.0), writes=["kTb0", "kTb1"])

    wstate = {"n": 0}

    def next_w(l, idx):
        s = wstate["n"] % NSLOT
        wstate["n"] += 1
        key = "W%d" % s
        P.dma("pool", lambda e: e.dma_start(out=slots[s][:], in_=T("ws")[l, idx]), reads=[], writes=[key])
        return slots[s], key

    def gain(l, i, c):
        return lp[:, l, PL_G + i * 8 + c:PL_G + i * 8 + c + 1]

    sqi = {"n": 0}

    def rmsnorm(l, gi, c0, n, si, bank=7):
        bk = "B%d" % bank
        for c in range(8):
            k = sqi["n"] % 2
            sqi["n"] += 1
            if c % 3 == 2:
                TT(sq[k][:, 0:n], x[:, c, c0:c0 + n], x[:, c, c0:c0 + n], ALU.mult, ["x%d_%d" % (c, si)], ["sq%d" % k])
            else:
                ACT(sq[k][:, 0:n], x[:, c, c0:c0 + n], AF.Square, ["x%d_%d" % (c, si)], ["sq%d" % k])
            MM(B[bank][:, 0:n], onesb[:], sq[k][:, 0:n], c == 0, c == 7, ["sq%d" % k, "onesb"], [bk])
        rt, rk = scratch()[0:2]
        ACT(rt[:, 0:n], B[bank][:, 0:n], AF.Ln, [bk], [rk], scale=1.0 / D, bias=epst[:, 0:1])
        ACT(rt[:, 0:n], rt[:, 0:n], AF.Exp, [rk], [rk], scale=-0.5)
        for c in range(8):
            STT(xn[:, c, c0:c0 + n], x[:, c, c0:c0 + n], gain(l, gi, c), rt[:, 0:n], ALU.mult, ALU.mult,
                ["x%d_%d" % (c, si), rk, "lp"], ["xn%d_%d" % (c, si)])

    def ffn(l, which, subs):
        base = 0 if which == 1 else 103
        gi = 0 if which == 1 else 3
        fence()
        for si, (c0, n) in enumerate(subs):
            rmsnorm(l, gi, c0, n, si)
        it = 0
        for j in range(NJ):
            wgt, wgk = next_w(l, base + 2 * j)
            wut, wuk = next_w(l, base + 2 * j + 1)
            for si, (c0, n) in enumerate(subs):
                pb = (it % 2) * 2
                it += 1
                for c in range(8):
                    MM(B[pb][:, 0:n], wgt[:, c * 128:(c + 1) * 128], xn[:, c, c0:c0 + n], c == 0, c == 7,
                       [wgk, "xn%d_%d" % (c, si)], ["B%d" % pb])
                for c in range(8):
                    MM(B[pb + 1][:, 0:n], wut[:, c * 128:(c + 1) * 128], xn[:, c, c0:c0 + n], c == 0, c == 7,
                       [wuk, "xn%d_%d" % (c, si)], ["B%d" % (pb + 1)])
                k = it % 2
                ACT(sg[k][:, 0:n], B[pb][:, 0:n], AF.Silu, ["B%d" % pb], ["sg%d" % k])
                TT(hb[:, j, c0:c0 + n], sg[k][:, 0:n], B[pb + 1][:, 0:n], ALU.mult,
                   ["sg%d" % k, "B%d" % (pb + 1), "HBF"], ["h%d_%d" % (j, si)])
        for dc in range(8):
            for pc in range(3):
                wt, wk = next_w(l, base + 44 + dc * 3 + pc)
                nj = 8 if pc < 2 else 6
                for si, (c0, n) in enumerate(subs):
                    for jj in range(nj):
                        j = pc * 8 + jj
                        MM(B[4 + si][:, 0:n], wt[:, jj * 128:(jj + 1) * 128], hb[:, j, c0:c0 + n],
                           j == 0, j == NJ - 1, [wk, "h%d_%d" % (j, si), "HBF"], ["B%d" % (4 + si)])
            for si, (c0, n) in enumerate(subs):
                STT(x[:, dc, c0:c0 + n], B[4 + si][:, 0:n], 0.5, x[:, dc, c0:c0 + n], ALU.mult, ALU.add,
                    ["B%d" % (4 + si), "x%d_%d" % (dc, si)], ["x%d_%d" % (dc, si)])


    def negb(hc, c0, n):
        if c0 < TS:
            return stg[hc][:, c0:c0 + n], ["stg%d" % hc]
        return nbs[:, hc, 0:n], ["nbs"]

    fmb = {"n": 0}

    def fm_bank():
        fmb["n"] += 1
        return 0 if fmb["n"] % 2 else 4

    def proj_fm(wt, wk, si, c0, n, bank, m=128):
        for c in range(8):
            MM(B[bank][0:m, 0:n], wt[:, c * 128:c * 128 + m], xn[:, c, c0:c0 + n], c == 0, c == 7,
               [wk, "xn%d_%d" % (c, si)], ["B%d" % bank])

    tmb = {"n": 0}

    def tm_bank():
        tmb["n"] += 1
        return 1 if tmb["n"] % 2 else 3

    def proj_tm(wt, wk, si, col0, rows, bank, boff):
        for c in range(8):
            MM(B[bank][0:rows, boff:boff + 128], xn[:, c, col0:col0 + rows], wt[:, c * 128:(c + 1) * 128], c == 0, c == 7,
               [wk, "xn%d_%d" % (c, si)], ["B%d" % bank])

    def qknorm_rope(l, qk, pb, c0, n, out_bf, out_keys):
        bk = "B%d" % pb
        gcol = PL_QN if qk == 0 else PL_KN
        rt, rk, sa, ka, sb, kb = scratch()
        ACT(sa[:, 0:n], B[pb][:, 0:n], AF.Copy, [bk], [ka])
        k = sqi["n"] % 2
        sqi["n"] += 1
        ACT(sq[k][:, 0:n], B[pb][:, 0:n], AF.Square, [bk], ["sq%d" % k])
        MM(B[7][:, 0:n], bd64b[:], sq[k][:, 0:n], True, True, ["sq%d" % k, "cm%d" % C_BD64], ["B7"])
        MM(B[6][:, 0:n], rot[:, l, qk, :], sa[:, 0:n], True, True, [ka, "rot"], ["B6"])
        ACT(rt[:, 0:n], B[7][:, 0:n], AF.Ln, ["B7"], [rk], scale=1.0 / 64, bias=epst[:, 0:1])
        ACT(rt[:, 0:n], rt[:, 0:n], AF.Exp, [rk], [rk], scale=-0.5)
        STT(sb[:, 0:n], sa[:, 0:n], lp[:, l, gcol:gcol + 1], rp[:, 0, c0:c0 + n], ALU.mult, ALU.mult,
            [ka, "lp", "rp"], [kb])
        TT(sa[:, 0:n], B[6][:, 0:n], rp[:, 1, c0:c0 + n], ALU.mult, ["B6", "rp", ka], [ka])
        TT(sb[:, 0:n], sb[:, 0:n], sa[:, 0:n], ALU.add, [ka, kb], [kb])
        TT(sb[:, 0:n], sb[:, 0:n], rt[:, 0:n], ALU.mult, [kb, rk], [kb])
        ACT(out_bf, sb[:, 0:n], AF.Copy, [kb, "HBF"], out_keys)
        return sb, kb

    def swa_block(l, st, blk, part):
        q0 = blk * 128
        si = blk // 4
        first = (st == 0 and blk == 0)
        kbs = ([] if first else [(0, blk, mprev)]) + [(1, blk + 1, mown)]
        if part[0] == "s":
            g = part[1]
            gp = slice(g * 64, (g + 1) * 64)
            for ii, (own, vb, mask) in enumerate(kbs):
                bank = own
                for h in range(4):
                    MM(B[bank][:, h * 128:(h + 1) * 128], kTb[gp, l, vb * 128:(vb + 1) * 128], qT[gp, h, q0:q0 + 128],
                       True, True, ["kTb%d" % l, "qT%d_%d" % (h, si), "HBF"], ["B%d" % bank])
                ACT(pbuf[own][:], B[bank][:], AF.Exp, ["B%d" % bank], ["pb%d" % own], scale=0.125)
                TT(pbuf[own][:], pbuf[own][:], mask[:], ALU.mult, ["pb%d" % own, "cm%d" % (C_OWN if own else C_PREV)],
                   ["pb%d" % own])
        elif part[0] == "p":
            g = part[1]
            gp = slice(g * 64, (g + 1) * 64)
            for ii, (own, vb, mask) in enumerate(kbs):
                MM(B[2][gp, :], Vb[:, l, vb, g * 64:(g + 1) * 64], pbuf[own][:], ii == 0, ii == len(kbs) - 1,
                   ["Vb%d" % l, "pb%d" % own], ["B2"])
            for ii, (own, vb, mask) in enumerate(kbs):
                MM(B[3][gp, :], onesb[:, 0:64], pbuf[own][:], ii == 0, ii == len(kbs) - 1,
                   ["onesb", "pb%d" % own], ["B3"])
        else:
            dnv = dn[:].rearrange("p (h q) -> p h q", h=4)
            TT(dnv, B[3][:].rearrange("p (h q) -> p h q", h=4), esk[:, l, :].unsqueeze(2).to_broadcast([128, 4, 128]), ALU.add,
               ["B3", "esk"], ["dn"])
            ACT(dn[:], dn[:], AF.Ln, ["dn"], ["dn"])
            ACT(dn[:], dn[:], AF.Exp, ["dn"], ["dn"], scale=-1.0)
            TT(xn[:, 0:4, q0:q0 + 128], B[2][:].rearrange("p (h q) -> p h q", h=4), dnv, ALU.mult,
               ["B2", "dn"], ["xn%d_%d" % (m, si) for m in range(4)])

    def gla_block(l, bi, col0, rows, si, maskt, maskkey, sample=False, part=None):
        cs = slice(col0, col0 + rows)
        for h in range(4):
            if part not in (None, "A"):
                break
            hc, hp = h // 2, (h % 2) * 64
            bank = 6 + (h % 2)
            MM(B[bank][0:rows, hc * 128:hc * 128 + rows], kg[hp:hp + 64, hc, cs], qg[hp:hp + 64, hc, cs], True, True,
               ["kg%d_%d" % (hc, si), "qg%d_%d" % (hc, si), "HBF"], ["B%d" % bank])
        for par in range(2):
            if part not in (None, "A"):
                break
            bank = 6 + par
            for hc in range(2):
                h = 2 * hc + par
                TT(Asb[0:rows, h, 0:rows], B[bank][0:rows, hc * 128:hc * 128 + rows], maskt[0:rows, 0:rows], ALU.mult,
                   ["B%d" % bank, maskkey], ["Asb"])
        if not sample:
            for h in range(4):
                if part not in (None, "IO"):
                    break
                hc, hp = h // 2, (h % 2) * 64
                MM(B[4][:, h * 128:h * 128 + rows], Sbf[hp:hp + 64, l, hc, :], qg[hp:hp + 64, hc, cs], True, False,
                   ["Sbf%d" % l, "qg%d_%d" % (hc, si), "HBF"], ["B4"])
                MM(B[4][:, h * 128:h * 128 + rows], Vg[0:128, bi, h * 128:(h + 1) * 128], Asb[:, h, 0:rows], False, True,
                   ["Vg%d" % bi, "Asb", "HBF"], ["B4"])
            if part not in (None, "S"):
                return
            for h in range(4):
                hc, hp = h // 2, (h % 2) * 64
                MM(B[5][hp:hp + 64, hc * 128:(hc + 1) * 128], ktok[0:128, bi, h * 64:(h + 1) * 64],
                   Vg[0:128, bi, h * 128:(h + 1) * 128], True, True, ["ktok%d" % bi, "Vg%d" % bi, "HBF"], ["B5"])
            for hc in range(2):
                TT(dn[:, hc * 128:(hc + 1) * 128], B[5][:, hc * 128:(hc + 1) * 128], Sst[:, l, hc, :], ALU.add,
                   ["B5", "Sst%d" % l], ["dn"])
                P.op("dve", lambda e, hc=hc: e.tensor_scalar(out=Sst[:, l, hc, :], in0=dn[:, hc * 128:(hc + 1) * 128],
                                                            scalar1=ebl[:, hc, bi:bi + 1], scalar2=None, op0=ALU.mult),
                     reads=["dn", "ebl"], writes=["Sst%d" % l])
            ACT(Sbf[:, l, :, :], Sst[:, l, :, :], AF.Copy, ["Sst%d" % l], ["Sbf%d" % l])

    def gla_out(l, col0, rows, si):
        n = 512
        k = sqi["n"] % 2
        sqi["n"] += 1
        ACT(sq[k][:], B[4][:], AF.Square, ["B4"], ["sq%d" % k])
        MM(B[5][:], onesb[:], sq[k][:], True, True, ["sq%d" % k, "onesb"], ["B5"])
        rt, rk, sa, ka, sb, kb = scratch()
        ACT(rt[:], B[5][:], AF.Ln, ["B5"], [rk], scale=1.0 / 128, bias=epst[:, 0:1])
        ACT(rt[:], rt[:], AF.Exp, [rk], [rk], scale=-0.5)
        STT(sa[:], B[4][:], lp[:, l, PL_ON:PL_ON + 1], rt[:], ALU.mult, ALU.mult, ["B4", "lp", rk], [ka])
        TT(xn[:, 4:8, col0:col0 + rows], sa[:].rearrange("p (h q) -> p h q", h=4)[:, :, 0:rows],
           gate[:, :, col0:col0 + rows], ALU.mult, [ka, "HBF"] + ["gate%d_%d" % (h, si) for h in range(4)],
           ["xn%d_%d" % (m, si) for m in range(4, 8)])

    def mixer(l, st, subs, last):
        base = 68
        fence()
        for si, (c0, n) in enumerate(subs):
            rmsnorm(l, 1, c0, n, si)
        nblk = 8
        wt, wk = next_w(l, base + 0)
        for si, (c0, n) in enumerate(subs):
            fb = fm_bank()
            proj_fm(wt, wk, si, c0, n, fb)
            kres, kresk = qknorm_rope(l, 1, fb, c0, n, kTb[:, l, 128 + c0:128 + c0 + n], ["kTb%d" % l])
            if last and si >= 1:
                off = 384 if si == 1 else 0
                rows = 128 if si == 1 else NS_TOK
                TR(B[7][0:rows, 0:128], kres[:, off:off + rows], [kresk], ["B7"])
                ACT(ko[0:rows, :], B[7][0:rows, 0:128], AF.Copy, ["B7"], ["ko"])
                if si == 1:
                    DMA(T("o_swak")[l], ko[:, :], ["ko"], ["out_ko"], key="out_ko")
                else:
                    for sq_ in range(NSEQ_S):
                        DMA(T("o_swaks")[l, sq_, 124:128, :], ko[4 * sq_:4 * sq_ + 4, :], ["ko"], ["out_ko"], key="out_ko")
        wt, wk = next_w(l, base + 1)
        for si, (c0, n) in enumerate(subs):
            rows = min(n, 128)
            nb = n // rows
            tb = tm_bank()
            for b_ in range(nb):
                proj_tm(wt, wk, si, c0 + b_ * rows, rows, tb, b_ * 128)
            blk0 = 1 + c0 // 128
            ACT(Vb[0:rows, l, blk0:blk0 + nb, :], B[tb][0:rows, 0:nb * 128].rearrange("p (b f) -> p b f", b=nb), AF.Copy,
                ["B%d" % tb], ["Vb%d" % l])
            if last and si >= 1:
                off = 384 if si == 1 else 0
                COPY(vo[0:rows, :], B[tb][0:rows, off:off + 128], ["B%d" % tb], ["vo"])
                if si == 1:
                    DMA(T("o_swav")[l], vo[:, :], ["vo"], ["out_vo"], key="out_vo")
                else:
                    for sq_ in range(NSEQ_S):
                        DMA(T("o_swavs")[l, sq_, 124:128, :], vo[4 * sq_:4 * sq_ + 4, :], ["vo"], ["out_vo"], key="out_vo")
        for c_ in range(4):
            wt, wk = next_w(l, base + 2 + c_)
            for si, (c0, n) in enumerate(subs):
                fb = fm_bank()
                proj_fm(wt, wk, si, c0, n, fb)
                qknorm_rope(l, 0, fb, c0, n, qT[:, c_, c0:c0 + n], ["qT%d_%d" % (c_, si)])
        wt, wk = next_w(l, base + 6)
        for si, (c0, n) in enumerate(subs):
            proj_fm(wt, wk, si, c0, n, 0, m=16)
            ACT(lrT[0:16, c0:c0 + n], B[0][0:16, 0:n], AF.Copy, ["B0"], ["lrT%d" % si])
            rows = min(n, 128)
            for b_ in range(n // rows):
                col0 = c0 + b_ * rows
                MM(B[1][0:rows, 0:256], lrT[0:32, col0:col0 + rows], wgb[:, l, :], True, True, ["lrT%d" % si, "wgb"], ["B1"])
                ACT(lsp[0:rows, :], B[1][0:rows, 0:256], AF.Exp, ["B1"], ["lsp"], scale=-1.0)
                ACT(lsp[0:rows, :], lsp[0:rows, :], AF.Ln, ["lsp"], ["lsp"], bias=1.0)
                tri = cst[0:rows, C_TRI:C_TRI + rows] if rows == 128 else cst[0:rows, C_TRIS:C_TRIS + rows]
                for hc in range(2):
                    MM(B[2][:, hc * 128:hc * 128 + rows], lsp[0:rows, hc * 128:(hc + 1) * 128], tri, True, True,
                       ["lsp", "cst"], ["B2"])
                    dst, dk = negb(hc, col0, rows)
                    ACT(dst, B[2][:, hc * 128:hc * 128 + rows], AF.Copy, ["B2"], dk + ["nb%d_%d" % (hc, si)])
            for hc in range(2):
                src, sk = negb(hc, c0, n)
                if n >= 128:
                    nb = n // 128
                    ACT(ebl[:, hc, (c0 // 128):(c0 // 128) + nb], src[:, 127:n:128], AF.Exp, sk + ["nb%d_%d" % (hc, si)], ["ebl"],
                        scale=-1.0 / 16)
                else:
                    ACT(ebl[:, hc, 9:9 + NSEQ_S], src[:, 3:n:4], AF.Exp, sk + ["nb%d_%d" % (hc, si)], ["ebl"], scale=-1.0 / 16)
        for hc in range(2):
            wt, wk = next_w(l, base + 7 + hc)
            for si, (c0, n) in enumerate(subs):
                fb = fm_bank()
                proj_fm(wt, wk, si, c0, n, fb)
                src, sk = negb(hc, c0, n)
                ACT(sg[0][:, 0:n], src, AF.Exp, sk + ["nb%d_%d" % (hc, si)], ["sg0"], scale=-1.0 / 16)
                STT(qg[:, hc, c0:c0 + n], B[fb][:, 0:n], 0.125, sg[0][:, 0:n], ALU.mult, ALU.mult, ["B%d" % fb, "sg0", "HBF"],
                    ["qg%d_%d" % (hc, si)])
        for hc in range(2):
            wt, wk = next_w(l, base + 9 + hc)
            for si, (c0, n) in enumerate(subs):
                proj_fm(wt, wk, si, c0, n, 0)
                src, sk = negb(hc, c0, n)
                ACT(sg[0][:, 0:n], src, AF.Exp, sk + ["nb%d_%d" % (hc, si)], ["sg0"], scale=1.0 / 16)
                TT(sg[1][:, 0:n], B[0][:, 0:n], sg[0][:, 0:n], ALU.mult, ["B0", "sg0"], ["sg1"])
                ACT(kg[:, hc, c0:c0 + n], sg[1][:, 0:n], AF.Copy, ["sg1", "HBF"], ["kg%d_%d" % (hc, si)])
                rows = min(n, 128)
                nb = n // rows
                for b_ in range(nb):
                    TR(B[1][0:rows, b_ * 128:(b_ + 1) * 128], sg[1][:, b_ * rows:(b_ + 1) * rows], ["sg1"], ["B1"])
                bi0 = c0 // 128
                ACT(ktok[0:rows, bi0:bi0 + nb, hc * 128:(hc + 1) * 128],
                    B[1][0:rows, 0:nb * 128].rearrange("p (b f) -> p b f", b=nb), AF.Copy, ["B1", "HBF"],
                    ["ktok%d" % (bi0 + b_) for b_ in range(nb)])
        for h in range(4):
            wt, wk = next_w(l, base + 11 + h)
            for si, (c0, n) in enumerate(subs):
                rows = min(n, 128)
                nb = n // rows
                tb = tm_bank()
                for b_ in range(nb):
                    proj_tm(wt, wk, si, c0 + b_ * rows, rows, tb, b_ * 128)
                bi0 = c0 // 128
                ACT(Vg[0:rows, bi0:bi0 + nb, h * 128:(h + 1) * 128],
                    B[tb][0:rows, 0:nb * 128].rearrange("p (b f) -> p b f", b=nb), AF.Copy, ["B%d" % tb, "HBF"],
                    ["Vg%d" % (bi0 + b_) for b_ in range(nb)])
        for h in range(4):
            wt, wk = next_w(l, base + 15 + h)
            for si, (c0, n) in enumerate(subs):
                fb = fm_bank()
                proj_fm(wt, wk, si, c0, n, fb)
                ACT(gate[:, h, c0:c0 + n], B[fb][:, 0:n], AF.Silu, ["B%d" % fb, "HBF"], ["gate%d_%d" % (h, si)])
        for blk in range(nblk):
            ga = (l, blk, blk * 128, 128, blk // 4, mown, "cm%d" % C_OWN)
            swa_block(l, st, blk, ("s", 0))
            gla_block(*ga, part="A")
            swa_block(l, st, blk, ("p", 0))
            swa_block(l, st, blk, ("s", 1))
            gla_block(*ga, part="IO")
            swa_block(l, st, blk, ("p", 1))
            gla_block(*ga, part="S")
            swa_block(l, st, blk, ("f",))
            gla_out(l, blk * 128, 128, blk // 4)
        COPY(kTb[:, l, 0:128], kTb[:, l, TS:TS + 128], ["kTb%d" % l], ["kTb%d" % l])
        COPY(Vb[:, l, 0, :], Vb[:, l, 8, :], ["Vb%d" % l], ["Vb%d" % l])
        if last:
            if stage >= 4 or stage == -3:
                sample_mix(l)
            for hc in range(2):
                for hh in range(2):
                    DMA(T("o_gla")[l, 2 * hc + hh], Sst[hh * 64:(hh + 1) * 64, l, hc, :], ["Sst%d" % l], ["out_gla"], key="out_gla")
        for dc in range(8):
            wt, wk = next_w(l, base + 19 + dc)
            for si, (c0, n) in enumerate(subs):
                for m in range(8):
                    MM(B[si][:, 0:n], wt[:, m * 128:(m + 1) * 128], xn[:, m, c0:c0 + n], m == 0, m == 7,
                       [wk, "xn%d_%d" % (m, si)], ["B%d" % si])
                TT(x[:, dc, c0:c0 + n], B[si][:, 0:n], x[:, dc, c0:c0 + n], ALU.add,
                   ["B%d" % si, "x%d_%d" % (dc, si)], ["x%d_%d" % (dc, si)])
        fence()


    def next_w_pre(l, i):
        s_ = wstate["n"] % NSLOT
        wstate["n"] += 1
        key = "W%d" % s_
        P.dma("pool", lambda e: e.dma_start(out=slots[s_][:], in_=T("wpre")[l, i]), reads=[], writes=[key])
        return slots[s_], key

    def memkv():
        for blk in range(2):
            load_x(T("mem")[blk * 128:(blk + 1) * 128, :], 128, blk * 128, lambda c: 0)
        for l in range(2):
            rmsnorm(l, 4, 0, 256, 0)
            for kv in range(2):
                for h in range(4):
                    wt, wk = next_w_pre(l, kv * 4 + h)
                    for mt in range(2):
                        proj_tm(wt, wk, 0, mt * 128, 128, mt, h * 128)
                for mt in range(2):
                    bk = "B%d" % mt
                    if kv == 0:
                        for h in range(4):
                            ACT(sg[0][:, h * 128:(h + 1) * 128], B[mt][:, h * 128:(h + 1) * 128], AF.Square, [bk], ["sg0", "ss4"],
                                accum_out=ss4[:, h:h + 1])
                        ACT(ss4[:, 4:8], ss4[:, 0:4], AF.Sqrt, ["ss4"], ["ss4"], scale=1.0 / 128, bias=EPS)
                        RECIP(ss4[:, 4:8], ss4[:, 4:8], ["ss4"], ["ss4"])
                        TT(sg[0][:].rearrange("p (h f) -> p h f", h=4), B[mt][:].rearrange("p (h f) -> p h f", h=4),
                           ss4[:, 4:8].unsqueeze(2).to_broadcast([128, 4, 128]), ALU.mult, [bk, "ss4"], ["sg0"])
                        TT(sg[1][:].rearrange("p (h f) -> p h f", h=4), sg[0][:].rearrange("p (h f) -> p h f", h=4),
                           lp[:, l, PL_XK:PL_XK + 128].unsqueeze(1).to_broadcast([128, 4, 128]), ALU.mult, ["sg0", "lp"], ["sg1"])
                        DMA(T("o_memk")[l, mt * 128:(mt + 1) * 128, :], sg[1][:], ["sg1"], ["out_mk"], key="out_mk")
                        for h in range(4):
                            TR(B[2][:, h * 128:(h + 1) * 128], sg[1][:, h * 128:(h + 1) * 128], ["sg1"], ["B2"])
                        ACT(kTm[:, l, :, mt * 128:(mt + 1) * 128], B[2][:].rearrange("p (h m) -> p h m", h=4), AF.Copy,
                            ["B2"], ["kTm%d" % l])
                    else:
                        ACT(sg[1][:], B[mt][:], AF.Copy, [bk], ["sg1"])
                        COPY(Vm[:, l, mt, :], B[mt][:], [bk], ["Vm%d" % l])
                        DMA(T("o_memv")[l, mt * 128:(mt + 1) * 128, :], sg[1][:], ["sg1"], ["out_mv"], key="out_mv")

    XS = 128.0 ** -0.5

    def xattn(l, st, subs, last):
        base = 95
        fence()
        for si, (c0, n) in enumerate(subs):
            rmsnorm(l, 2, c0, n, si)
        for h in range(4):
            wt, wk = next_w(l, base + h)
            for si, (c0, n) in enumerate(subs):
                proj_fm(wt, wk, si, c0, n, 0)
                k = sqi["n"] % 2
                sqi["n"] += 1
                ACT(sq[k][:, 0:n], B[0][:, 0:n], AF.Square, ["B0"], ["sq%d" % k])
                MM(B[7][:, 0:n], onesb[:], sq[k][:, 0:n], True, True, ["sq%d" % k, "onesb"], ["B7"])
                rt, rk = scratch()[0:2]
                ACT(rt[:, 0:n], B[7][:, 0:n], AF.Ln, ["B7"], [rk], scale=1.0 / 128, bias=epst[:, 0:1])
                ACT(rt[:, 0:n], rt[:, 0:n], AF.Exp, [rk], [rk], scale=-0.5)
                if c0 >= TS:
                    STT(qxs[:, h, 0:n], B[0][:, 0:n], lp[:, l, PL_XQ:PL_XQ + 1], rt[:, 0:n], ALU.mult, ALU.mult,
                        ["B0", "lp", rk], ["qxs%d" % h])
                else:
                    STT(qT[:, h, c0:c0 + n], B[0][:, 0:n], lp[:, l, PL_XQ:PL_XQ + 1], rt[:, 0:n], ALU.mult, ALU.mult,
                        ["B0", "lp", rk, "HBF"], ["qT%d_%d" % (h, si)])
        for si, (c0, n) in enumerate(subs[0:2]):
            for h in range(4):
                for mt in range(2):
                    MM(B[mt][:, 0:n], kTm[:, l, h, mt * 128:(mt + 1) * 128], qT[:, h, c0:c0 + n], True, True,
                       ["kTm%d" % l, "qT%d_%d" % (h, si), "HBF"], ["B%d" % mt])
                    ACT(pbuf[mt][:, 0:n], B[mt][:, 0:n], AF.Exp, ["B%d" % mt], ["pb%d" % mt], scale=XS)
                for mt in range(2):
                    MM(B[2][:, 0:n], Vm[:, l, mt, h * 128:(h + 1) * 128], pbuf[mt][:, 0:n], mt == 0, mt == 1,
                       ["Vm%d" % l, "pb%d" % mt], ["B2"])
                for mt in range(2):
                    MM(B[3][:, 0:n], onesb[:], pbuf[mt][:, 0:n], mt == 0, mt == 1, ["onesb", "pb%d" % mt], ["B3"])
                ACT(dn[:, 0:n], B[3][:, 0:n], AF.Ln, ["B3"], ["dn"])
                ACT(dn[:, 0:n], dn[:, 0:n], AF.Exp, ["dn"], ["dn"], scale=-1.0)
                TT(gate[:, h, c0:c0 + n], B[2][:, 0:n], dn[:, 0:n], ALU.mult, ["B2", "dn", "HBF"], ["gate%d_%d" % (h, si)])
        if last and (stage >= 5 or stage in (-2, -3, -4)):
            SC = TS
            P.op("pe", lambda e: e.drain(), reads=[], writes=[])
            MM(B[7][:], zb[:], mown[:], True, False, ["zb", "cm%d" % C_OWN], ["B7"])
            MM(B[3][:], zb[:], mown[:], True, False, ["zb", "cm%d" % C_OWN], ["B3"])
            for sq_ in range(NSEQ_S):
                DMA(stg[0][:].rearrange("p (mt f) -> p mt f", mt=2), T("c_memk")[l, sq_].rearrange("(mt p) f -> p mt f", p=128),
                    [], ["stg0"])
                DMA(stg[1][:].rearrange("p (mt f) -> p mt f", mt=2), T("c_memv")[l, sq_].rearrange("(mt p) f -> p mt f", p=128),
                    [], ["stg1"])
                for mt in range(2):
                    for h in range(4):
                        TR(B[4 + mt][:, h * 128:(h + 1) * 128], stg[0][:, mt * 512 + h * 128:mt * 512 + (h + 1) * 128], ["stg0"],
                           ["B%d" % (4 + mt)])
                    ACT(kTs[:, :, mt * 128:(mt + 1) * 128], B[4 + mt][:].rearrange("p (h m) -> p h m", h=4), AF.Copy,
                        ["B%d" % (4 + mt)], ["kTs"])
                COPY(Vs[:].rearrange("p mt f -> p (mt f)"), stg[1][:], ["stg1"], ["Vs"])
                for mt in range(2):
                    for h in range(4):
                        MM(B[0][:, (mt * 4 + h) * 64:(mt * 4 + h + 1) * 64], kTs[:, h, mt * 128:(mt + 1) * 128], qxs[:, h, :], True, True,
                           ["kTs", "qxs%d" % h], ["B0"])
                ACT(pbuf[0][:], B[0][:], AF.Exp, ["B0"], ["pb0"], scale=XS)
                TT(pbuf[0][:].rearrange("p (a t) -> p a t", a=8), pbuf[0][:].rearrange("p (a t) -> p a t", a=8),
                   msx[:, sq_, :].unsqueeze(1).to_broadcast([128, 8, 64]), ALU.mult, ["pb0", "msx"], ["pb0"])
                for h in range(4):
                    for mt in range(2):
                        MM(B[7][:, h * 64:(h + 1) * 64], Vs[:, mt, h * 128:(h + 1) * 128], pbuf[0][:, (mt * 4 + h) * 64:(mt * 4 + h + 1) * 64],
                           False, (sq_ == NSEQ_S - 1 and h == 3 and mt == 1), ["Vs", "pb0"], ["B7"])
                MM(B[3][:], onesb[:], pbuf[0][:], False, sq_ == NSEQ_S - 1, ["onesb", "pb0"], ["B3"])
            COPY(dn[:, 256:512], B[3][:, 256:512], ["B3"], ["dn"])
            TT(dn[:, 0:256], B[3][:, 0:256], dn[:, 256:512], ALU.add, ["B3", "dn"], ["dn"])
            RECIP(dn[:, 0:256], dn[:, 0:256], ["dn"], ["dn"])
            TT(gate[:, :, SC:SC + 64], B[7][:, 0:256].rearrange("p (h q) -> p h q", h=4), dn[:, 0:256].rearrange("p (h q) -> p h q", h=4),
               ALU.mult, ["B7", "dn", "HBF"], ["gate%d_2" % h for h in range(4)])
        for dcp in range(4):
            wt, wk = next_w(l, base + 4 + dcp)
            for dcc in range(2):
                dc = 2 * dcp + dcc
                for si, (c0, n) in enumerate(subs):
                    for m in range(4):
                        MM(B[si][:, 0:n], wt[:, (dcc * 4 + m) * 128:(dcc * 4 + m + 1) * 128], gate[:, m, c0:c0 + n], m == 0, m == 3,
                           [wk, "gate%d_%d" % (m, si), "HBF"], ["B%d" % si])
                    TT(x[:, dc, c0:c0 + n], B[si][:, 0:n], x[:, dc, c0:c0 + n], ALU.add,
                       ["B%d" % si, "x%d_%d" % (dc, si)], ["x%d_%d" % (dc, si)])
        fence()

    def sample_mix(l):
        SC = TS
        si = 2
        sample_swa(l, SC)
        sample_gla(l, SC)

    def sample_swa(l, SC):
        for g in range(2):
            gp = slice(g * 64, (g + 1) * 64)
            for h in range(4):
                MM(B[g][0:64, h * 64:(h + 1) * 64], kTb[gp, l, 128 + SC:128 + SC + 64], qT[gp, h, SC:SC + 64], True, True,
                   ["kTb%d" % l, "qT%d_2" % h, "HBF"], ["B%d" % g])
            ACT(pnew[:, g, :], B[g][0:64, 0:256], AF.Exp, ["B%d" % g], ["pnew"], scale=0.125)
        TT(pnew[:], pnew[:], msn[0:64, :].unsqueeze(1).to_broadcast([64, 2, 256]), ALU.mult, ["pnew", "cm%d" % C_SN], ["pnew"],
           eng="pool")
        for g in range(2):
            gp = slice(g * 64, (g + 1) * 64)
            MM(B[5][gp, 0:256], Vb[0:64, l, 9, g * 64:(g + 1) * 64], pnew[:, g, :], True, False, ["Vb%d" % l, "pnew"], ["B5"])
            MM(B[6][gp, 0:256], onesb[0:64, 0:64], pnew[:, g, :], True, False, ["onesb", "pnew"], ["B6"])
        DMA(T("o_swaks")[l, :, 0:124, :].rearrange("s k f -> s (k f)"), T("c_swak")[l, :, 4:128, :].rearrange("s k f -> s (k f)"),
            [], ["out_ck"], key="out_ck")
        DMA(T("o_swavs")[l, :, 0:124, :].rearrange("s k f -> s (k f)"), T("c_swav")[l, :, 4:128, :].rearrange("s k f -> s (k f)"),
            [], ["out_ck"], key="out_ck")
        for gi in range(4):
            DMA(Kst[:], T("c_swak")[l, 4 * gi:4 * gi + 4].rearrange("s k f -> k s f"), [], ["Kst"])
            P.dma("pool", lambda e, gi=gi: e.dma_start(out=Vc[:], in_=T("c_swav")[l, 4 * gi:4 * gi + 4].rearrange("s k f -> k s f")),
                  reads=[], writes=["Vc"])
            for j in range(4):
                TR(B[2][:, j * 128:(j + 1) * 128], Kst[:, j, :], ["Kst"], ["B2"])
            ACT(KcT[:], B[2][:].rearrange("p (j k) -> p j k", j=4), AF.Copy, ["B2"], ["KcT"])
            for g in range(2):
                gp = slice(g * 64, (g + 1) * 64)
                for j in range(4):
                    s_ = 4 * gi + j
                    for h in range(4):
                        MM(B[3 + g][:, h * 64 + 4 * s_:h * 64 + 4 * s_ + 4], KcT[gp, j, :], qT[gp, h, SC + 4 * s_:SC + 4 * s_ + 4],
                           True, True, ["KcT", "qT%d_2" % h, "HBF"], ["B%d" % (3 + g)])
                v_in = B[3 + g][:, 0:256].rearrange("p (h t) -> p h t", h=4)[:, :, 16 * gi:16 * gi + 16]
                v_out = pcach[:, g, :].rearrange("p (h t) -> p h t", h=4)[:, :, 16 * gi:16 * gi + 16]
                v_msk = msc[:].rearrange("p (h t) -> p h t", h=4)[:, :, 16 * gi:16 * gi + 16]
                ACT(v_out, v_in, AF.Exp, ["B%d" % (3 + g)], ["pcach"], scale=0.125)
                TT(v_out, v_out, v_msk, ALU.mult, ["pcach", "cm%d" % C_SC], ["pcach"], eng="pool")
            for g in range(2):
                gp = slice(g * 64, (g + 1) * 64)
                for j in range(4):
                    s_ = 4 * gi + j
                    lastmm = (gi == 3 and j == 3)
                    for h in range(4):
                        cc = slice(h * 64 + 4 * s_, h * 64 + 4 * s_ + 4)
                        MM(B[5][gp, cc], Vc[:, j, g * 64:(g + 1) * 64], pcach[:, g, cc], False, lastmm and h == 3, ["Vc", "pcach"], ["B5"])
                        MM(B[6][gp, cc], onesb[:, 0:64], pcach[:, g, cc], False, lastmm and h == 3, ["onesb", "pcach"], ["B6"])
        dnv = dn[:, 0:256].rearrange("p (h q) -> p h q", h=4)
        TT(dnv, B[6][:, 0:256].rearrange("p (h q) -> p h q", h=4), esk[:, l, :].unsqueeze(2).to_broadcast([128, 4, 64]), ALU.add,
           ["B6", "esk"], ["dn"])
        RECIP(dn[:, 0:256], dn[:, 0:256], ["dn"], ["dn"])
        TT(xn[:, 0:4, SC:SC + 64], B[5][:, 0:256].rearrange("p (h q) -> p h q", h=4), dnv, ALU.mult, ["B5", "dn"],
           ["xn%d_2" % m for m in range(4)])
    def sample_gla(l, SC):
        P.op("dve", lambda e: e.memset(Asb[64:128, :, :], 0.0), reads=[], writes=["Asb"])
        P.op("dve", lambda e: e.memset(Vg[64:128, 8, :], 0.0), reads=["HBF"], writes=["Vg8"])
        gla_block(l, 8, SC, 64, 2, mgs, "cm%d" % C_GMS, sample=True)
        P.op("dve", lambda e: e.memset(qz[:], 0.0), reads=[], writes=["qz"])
        COPY(qz[0:64, 0, :, :], qg[0:64, :, SC:SC + 64], ["qg0_2", "qg1_2", "HBF", "qz"], ["qz"])
        COPY(qz[64:128, 1, :, :], qg[64:128, :, SC:SC + 64], ["qg0_2", "qg1_2", "HBF", "qz"], ["qz"])
        MM(B[4][:], zb[:], mown[:], True, False, ["zb", "cm%d" % C_OWN], ["B4"])
        for h in range(4):
            MM(B[4][:, h * 128:h * 128 + 64], Vg[0:128, 8, h * 128:(h + 1) * 128], Asb[:, h, 0:64], False, False,
               ["Vg8", "Asb", "HBF"], ["B4"])
        for gi in range(4):
            for hh in range(2):
                for hc in range(2):
                    DMA(S0[hh * 64:(hh + 1) * 64, hc * 4:(hc + 1) * 4, :],
                        T("c_gla")[l, 4 * gi:4 * gi + 4, 2 * hc + hh].rearrange("s d v -> d s v"), ["HBF"], ["S0"])
            ACT(S0b[:], S0[:], AF.Copy, ["S0", "HBF"], ["S0b"])
            for h in range(4):
                hc, par = h // 2, h % 2
                for j in range(4):
                    s_ = 4 * gi + j
                    MM(B[4][:, h * 128 + 4 * s_:h * 128 + 4 * s_ + 4], S0b[:, hc * 4 + j, :], qz[:, par, hc, 4 * s_:4 * s_ + 4],
                       False, (gi == 3 and j == 3 and h == 3), ["S0b", "qz", "HBF"], ["B4"])
            TT(Kbl[:], ktok[0:64, 8, :].unsqueeze(1).to_broadcast([64, 4, 256]),
               bmb[:, 4 * gi:4 * gi + 4].unsqueeze(2).to_broadcast([64, 4, 256]), ALU.mult, ["ktok8", "bmb", "HBF"], ["Kbl"])
            for h in range(4):
                hc, hp = h // 2, (h % 2) * 64
                for j in range(4):
                    MM(B[5 + hc][hp:hp + 64, j * 128:(j + 1) * 128], Kbl[:, j, h * 64:(h + 1) * 64], Vg[0:64, 8, h * 128:(h + 1) * 128],
                       True, True, ["Kbl", "Vg8", "HBF"], ["B%d" % (5 + hc)])
            for hc in range(2):
                sv = S0[:, hc * 4:(hc + 1) * 4, :]
                TT(sv, B[5 + hc][:].rearrange("p (j v) -> p j v", j=4), sv, ALU.add, ["B%d" % (5 + hc), "S0", "HBF"], ["S0"])
                TT(sv, sv, ebl[:, hc, 9 + 4 * gi:9 + 4 * gi + 4].unsqueeze(2).to_broadcast([128, 4, 128]), ALU.mult,
                   ["S0", "ebl", "HBF"], ["S0"])
            for hh in range(2):
                for hc in range(2):
                    DMA(T("o_glas")[l, 4 * gi:4 * gi + 4, 2 * hc + hh].rearrange("s d v -> d s v"),
                        S0[hh * 64:(hh + 1) * 64, hc * 4:(hc + 1) * 4, :], ["S0", "HBF"], ["out_gs"], key="out_gs")
        gla_out(l, SC, 64, 2)

    def load_x(src, nrows, col0, si_of_col):
        k = load_x.n % 2
        load_x.n += 1
        DMA(stg[k][0:nrows, :], src, [], ["stg%d" % k])
        si = si_of_col(col0)
        for half in range(2):
            bank = 6 + half
            for cc in range(4):
                c = half * 4 + cc
                TR(B[bank][:, cc * 128:cc * 128 + nrows], stg[k][0:nrows, c * 128:(c + 1) * 128], ["stg%d" % k], ["B%d" % bank])
            src_v = B[bank][:].rearrange("p (c n) -> p c n", c=4)[:, :, 0:nrows]
            dst_v = x[:, half * 4:half * 4 + 4, col0:col0 + nrows]
            wk = ["x%d_%d" % (half * 4 + cc, si) for cc in range(4)]
            if half == 0:
                ACT(dst_v, src_v, AF.Copy, ["B%d" % bank], wk)
            else:
                COPY(dst_v, src_v, ["B%d" % bank], wk)
    load_x.n = 0
    load_x.n2 = 0

    def store_x(dst, nrows, col0, si_of_col):
        k = load_x.n % 2
        load_x.n += 1
        si = si_of_col(col0)
        for half in range(2):
            bank = 6 + (load_x.n2 % 2)
            load_x.n2 += 1
            for cc in range(4):
                c = half * 4 + cc
                TR(B[bank][0:nrows, cc * 128:(cc + 1) * 128], x[:, c, col0:col0 + nrows], ["x%d_%d" % (c, si)], ["B%d" % bank])
            if half == 0:
                ACT(stg[k][0:nrows, 0:512], B[bank][0:nrows, :], AF.Copy, ["B%d" % bank], ["stg%d" % k])
            else:
                COPY(stg[k][0:nrows, 512:1024], B[bank][0:nrows, :], ["B%d" % bank], ["stg%d" % k])
        DMA(dst, stg[k][0:nrows, :], ["stg%d" % k], ["out_stg%d" % k], key="out_stg%d" % k)

    if stage >= 3 or stage in (-2, -3, -4):
        memkv()
    for st in (sts if sts is not None else range(n_st)):
        last = (st == 3)
        subs = [(0, 512), (512, 512)] + ([(1024, NS_TOK)] if last else [])

        def si_of_col(col):
            return col // 512

        for blk in range(8):
            load_x(T("xp")[st * TS + blk * 128: st * TS + (blk + 1) * 128, :], 128, blk * 128, si_of_col)
        if last:
            load_x(T("xs")[:, :], NS_TOK, 1024, si_of_col)
        DMA(rp[:, 0, 0:TS], T("rope")[0, :, st * TS:(st + 1) * TS], [], ["rp"])
        DMA(rp[:, 1, 0:TS], T("rope")[1, :, st * TS:(st + 1) * TS], ["rp"], ["rp"])
        if last:
            DMA(rp[:, 0, TS:TMAX], T("rope")[0, :, SEQ:SEQ + NS_TOK], ["rp"], ["rp"])
            DMA(rp[:, 1, TS:TMAX], T("rope")[1, :, SEQ:SEQ + NS_TOK], ["rp"], ["rp"])
        for l in range(2):
            if stage == 0:
                break
            if stage == -2:
                xattn(l, st, subs, last)
                break
            if stage in (-3, -4):
                mixer(l, st, subs, last)
                xattn(l, st, subs, last)
                break
            if stage == -1:
                for si, (c0, n) in enumerate(subs):
                    rmsnorm(l, 0, c0, n, si)
                break
            ffn(l, 1, subs)
            if stage >= 2:
                mixer(l, st, subs, last)
            if stage >= 3:
                xattn(l, st, subs, last)
            if stage >= 5:
                ffn(l, 2, subs)
            if stage < 6:
                break
        for blk in range(8):
            store_x(T("yp")[st * TS + blk * 128: st * TS + (blk + 1) * 128, :], 128, blk * 128, si_of_col)
        if last:
            store_x(T("ys")[:, :], NS_TOK, 1024, si_of_col)

    P.emit()
    return nc, P, _decl


_OSPECS = {
    "yp": [SEQ, D], "ys": [NS_TOK, D], "o_swak": [2, 128, 128], "o_swav": [2, 128, 128],
    "o_gla": [2, 4, 64, 128], "o_memk": [2, 256, 512], "o_memv": [2, 256, 512],
    "o_swaks": [2, NSEQ_S, 128, 128], "o_swavs": [2, NSEQ_S, 128, 128], "o_glas": [2, NSEQ_S, 4, 64, 128],
}


def kernel(n_st=4, stage=99, sts=None, **inp):
    inp = {k: np.asarray(v) for k, v in inp.items()}
    wsarr = np.stack([_layer_stream(inp, l) for l in range(2)])
    wprearr = np.stack([np.stack([_colchunk(inp["xa_wk"][l], range(h * 128, (h + 1) * 128)) for h in range(4)] +
                                 [_colchunk(inp["xa_wv"][l], range(h * 128, (h + 1) * 128)) for h in range(4)])
                        for l in range(2)])
    consts = _consts()
    lpar = np.stack([_layer_params(inp, l) for l in range(2)])
    rope = _rope_tables()
    nc, P, decl = build(n_st=n_st, stage=stage, sts=sts)
    REAL = [0, 1, 4, 5]
    zx = np.zeros((SEQ, D), np.float32)
    zm = np.zeros((256, D), np.float32)
    in_maps = []
    for c in range(8):
        b = REAL.index(c) if c in REAL else None
        s0 = c * NSEQ_S
        in_maps.append({
            "xp": np.ascontiguousarray(inp["x_prompt"][b]) if b is not None else zx,
            "xs": np.ascontiguousarray(inp["x_sample"][s0:s0 + NSEQ_S].reshape(NS_TOK, D)),
            "ws": wsarr, "wpre": wprearr, "consts": consts, "lpar": lpar, "rope": rope,
            "mem": np.ascontiguousarray(inp["mem_prompt"][b]) if b is not None else zm,
            "c_swak": np.ascontiguousarray(inp["cache_swa_k"][:, s0:s0 + NSEQ_S].reshape(2, NSEQ_S, 128, 128)),
            "c_swav": np.ascontiguousarray(inp["cache_swa_v"][:, s0:s0 + NSEQ_S].reshape(2, NSEQ_S, 128, 128)),
            "c_gla": np.ascontiguousarray(inp["state_gla"][:, s0:s0 + NSEQ_S]),
            "c_memk": np.ascontiguousarray(inp["cache_mem_k"][:, s0:s0 + NSEQ_S].reshape(2, NSEQ_S, 256, 512)),
            "c_memv": np.ascontiguousarray(inp["cache_mem_v"][:, s0:s0 + NSEQ_S].reshape(2, NSEQ_S, 256, 512)),
        })
    in_maps = [{k: v for k, v in m.items() if k in decl} for m in in_maps]
    res = run_bass_kernel_spmd(nc, in_maps, core_ids=list(range(8)))
    R = [dict(r) for r in res.results]
    for r in R:
        for k, shp in _OSPECS.items():
            if k not in r:
                r[k] = np.zeros(shp, np.float32)
    y_p = np.stack([R[REAL[b]]["yp"] for b in range(4)])
    y_s = np.concatenate([R[c]["ys"].reshape(NSEQ_S, 4, D) for c in range(8)])
    swak_p = np.stack([R[REAL[b]]["o_swak"] for b in range(4)], axis=1).reshape(2, 4, 128, 2, 64)
    swav_p = np.stack([R[REAL[b]]["o_swav"] for b in range(4)], axis=1).reshape(2, 4, 128, 2, 64)
    gla_p = np.stack([R[REAL[b]]["o_gla"] for b in range(4)], axis=1)
    memk_p = np.stack([R[REAL[b]]["o_memk"] for b in range(4)], axis=1).reshape(2, 4, 256, 4, 128)
    memv_p = np.stack([R[REAL[b]]["o_memv"] for b in range(4)], axis=1).reshape(2, 4, 256, 4, 128)
    swak_s = np.concatenate([R[c]["o_swaks"] for c in range(8)], axis=1).reshape(2, 128, 128, 2, 64)
    swav_s = np.concatenate([R[c]["o_swavs"] for c in range(8)], axis=1).reshape(2, 128, 128, 2, 64)
    gla_s = np.concatenate([R[c]["o_glas"] for c in range(8)], axis=1)
    return (y_p, y_s, swak_p, swav_p, gla_p, memk_p, memv_p, swak_s, swav_s, gla_s)
```

```python
import os
import numpy as np
from contextlib import ExitStack
import concourse.bass as bass
import concourse.mybir as mybir
from concourse.bass_utils import run_bass_kernel_spmd

F32 = mybir.dt.float32
BF16 = mybir.dt.bfloat16
ALU = mybir.AluOpType
AF = mybir.ActivationFunctionType
ENGS = ("pe", "dve", "act", "pool", "sp")

D = 1024
DFF = 2816
NJ = 22
SEQ = 4096
NSEQ_S = 16
NS_TOK = 64
EPS = 1e-6
NCH = 171
TS = 1024
NSLOT = 6


class Op:
    __slots__ = ("idx", "eng", "fn", "deps", "dma_key", "dma_cnt", "needed", "cnt", "waits")

    def __init__(self, idx, eng, fn):
        self.idx = idx
        self.eng = eng
        self.fn = fn
        self.deps = set()
        self.dma_key = None
        self.dma_cnt = 0
        self.needed = False
        self.cnt = 0
        self.waits = []


class Prog:
    def __init__(self, nc):
        self.nc = nc
        self.ops = []
        self.last_w = {}
        self.readers = {}
        self.dma_counts = {}
        self.es = ExitStack()

    def sbuf(self, name, shape, dtype):
        return self.es.enter_context(self.nc.sbuf_tensor(name, list(shape), dtype))

    def psum(self, name, shape, dtype=F32):
        return self.es.enter_context(self.nc.psum_tensor(name, list(shape), dtype))

    def _rec(self, eng, fn, reads, writes):
        ex = [k for k in reads if isinstance(k, str) and k[0] == "B" and k[1:].isdigit()]
        if ex:
            reads = [k for k in reads if k not in ex]
            writes = list(writes) + [k for k in ex if k not in writes]
        op = Op(len(self.ops), eng, fn)
        for k in reads:
            w = self.last_w.get(k)
            if w is not None:
                op.deps.add(w)
        for k in writes:
            w = self.last_w.get(k)
            if w is not None:
                op.deps.add(w)
            for r in self.readers.get(k, ()):
                op.deps.add(r)
        for k in reads:
            self.readers.setdefault(k, []).append(op.idx)
        for k in writes:
            self.last_w[k] = op.idx
            self.readers[k] = []
        op.deps.discard(op.idx)
        self.ops.append(op)
        return op

    def op(self, eng, fn, reads=(), writes=()):
        return self._rec(eng, fn, reads, writes)

    def dma(self, eng, fn, reads=(), writes=(), key=None):
        op = self._rec(eng, fn, reads, writes)
        if key is None:
            key = writes[0]
        op.dma_key = key
        self.dma_counts[key] = self.dma_counts.get(key, 0) + 1
        op.dma_cnt = self.dma_counts[key]
        return op

    def emit(self, final_prefix="out"):
        nc = self.nc
        ops = self.ops
        seen = {e: {} for e in ENGS}
        seen_dma = {e: {} for e in ENGS}
        for op in ops:
            per_eng = {}
            for d in op.deps:
                p = ops[d]
                if p.dma_key is not None:
                    cur = seen_dma[op.eng].get(p.dma_key, 0)
                    if p.dma_cnt > cur:
                        seen_dma[op.eng][p.dma_key] = p.dma_cnt
                        op.waits.append(("dma", p.dma_key, p.dma_cnt))
                    continue
                if p.eng == "pe" and op.eng == "pe":
                    continue
                if d > per_eng.get(p.eng, -1):
                    per_eng[p.eng] = d
            for pe_, d in per_eng.items():
                if d > seen[op.eng].get(pe_, -1):
                    seen[op.eng][pe_] = d
                    ops[d].needed = True
                    op.waits.append(("eng", pe_, d))
        cnts = {e: 0 for e in ENGS}
        for op in ops:
            if op.dma_key is None and op.needed:
                cnts[op.eng] += 1
                op.cnt = cnts[op.eng]
        sems = {e: self.es.enter_context(nc.semaphore("s_" + e)) for e in ENGS}
        dsems = {}
        for k in self.dma_counts:
            dsems[k] = self.es.enter_context(nc.semaphore("d%d" % len(dsems)))
        self.n_sems = len(sems) + len(dsems)
        by_eng = {e: [o for o in ops if o.eng == e] for e in ENGS}
        fin = [(dsems[k], 16 * self.dma_counts[k]) for k in self.dma_counts if str(k).startswith(final_prefix)]

        def run(ename, e):
            for op in by_eng[ename]:
                for w in op.waits:
                    if w[0] == "dma":
                        e.wait_ge(dsems[w[1]], 16 * w[2])
                    else:
                        e.wait_ge(sems[w[1]], ops[w[2]].cnt)
                ins = op.fn(e)
                if op.dma_key is not None:
                    ins.then_inc(dsems[op.dma_key], 16)
                elif op.needed:
                    ins.then_inc(sems[ename], 1)
            if ename == "sp":
                for s, v in fin:
                    e.wait_ge(s, v)

        with nc.Block() as block:
            @block.tensor
            def _(e):
                run("pe", e)

            @block.vector
            def _(e):
                run("dve", e)

            @block.scalar
            def _(e):
                run("act", e)

            @block.gpsimd
            def _(e):
                run("pool", e)

            @block.sync
            def _(e):
                run("sp", e)
        self.es.close()


def _colchunk(W, cols):
    cols = np.asarray(list(cols))
    out = np.zeros((128, 8, 128), np.float32)
    sub = W[:, cols]
    out[:, :, :len(cols)] = sub.reshape(8, 128, len(cols)).transpose(1, 0, 2)
    return out.reshape(128, 1024)


def _ffn_chunks(wg, wu, wd):
    ch = []
    for j in range(NJ):
        ch.append(_colchunk(wg, range(j * 128, (j + 1) * 128)))
        ch.append(_colchunk(wu, range(j * 128, (j + 1) * 128)))
    wd3 = np.zeros((24 * 128, D), np.float32)
    wd3[:DFF] = wd
    wdr = wd3.reshape(24, 128, 8, 128)
    for dc in range(8):
        for pc in range(3):
            ch.append(np.ascontiguousarray(wdr[pc * 8:(pc + 1) * 8, :, dc, :].transpose(1, 0, 2)).reshape(128, 1024))
    return ch


def _win_groups():
    g = []
    g.append(list(range(512, 640)))
    g.append(list(range(640, 768)))
    for c in range(4):
        g.append(list(range(c * 64, c * 64 + 64)) + list(range((c + 4) * 64, (c + 4) * 64 + 64)))
    g.append(list(range(2304, 2320)))
    for hc in range(2):
        g.append(list(range(768 + hc * 128, 768 + (hc + 1) * 128)))
    for hc in range(2):
        g.append(list(range(1024 + hc * 128, 1024 + (hc + 1) * 128)))
    for h in range(4):
        g.append(list(range(1280 + h * 128, 1280 + (h + 1) * 128)))
    for h in range(4):
        g.append(list(range(1792 + h * 128, 1792 + (h + 1) * 128)))
    return g


def _mixrow(m, p):
    if m < 4:
        head = m if p < 64 else m + 4
        return head * 64 + (p % 64)
    return 512 + (m - 4) * 128 + p


def _layer_stream(inp, l):
    ch = _ffn_chunks(inp["ffn1_wg"][l], inp["ffn1_wu"][l], inp["ffn1_wd"][l])
    win = inp["w_in"][l]
    for cols in _win_groups():
        ch.append(_colchunk(win, cols))
    wout = inp["w_out"][l]
    rows = np.array([[_mixrow(m, p) for m in range(8)] for p in range(128)])
    for dc in range(8):
        ch.append(np.ascontiguousarray(wout[rows][:, :, dc * 128:(dc + 1) * 128]).reshape(128, 1024))
    wq = inp["xa_wq"][l]
    for h in range(4):
        ch.append(_colchunk(wq, range(h * 128, (h + 1) * 128)))
    wo = inp["xa_wo"][l].reshape(4, 128, 8, 128)
    for dcp in range(4):
        ch.append(np.ascontiguousarray(wo[:, :, 2 * dcp:2 * dcp + 2, :].transpose(1, 2, 0, 3)).reshape(128, 1024))
    ch += _ffn_chunks(inp["ffn2_wg"][l], inp["ffn2_wu"][l], inp["ffn2_wd"][l])
    assert len(ch) == NCH
    return np.stack(ch)


C_ID, C_ONES, C_BD64, C_ROT, C_TRI, C_TREV, C_GM, C_TRIS, C_TREVS, C_GMS = [i * 128 for i in range(10)]
C_OWN = 1280
C_PREV = C_OWN + 512
C_SC = C_PREV + 512
C_SN = C_SC + 256
C_BM = C_SN + 256
NCONST = C_BM + 16

PL_G = 0
PL_QN, PL_KN, PL_ON, PL_XQ = 40, 41, 42, 43
PL_SINK = 44
PL_XK = 48
PL_WG = 176
NPL = 176 + 256


def _consts():
    c = np.zeros((128, NCONST), np.float32)
    p = np.arange(128)
    c[:, C_ID:C_ID + 128] = np.eye(128)
    c[:, C_ONES:C_ONES + 128] = 1.0
    c[:, C_BD64:C_BD64 + 128] = (p[:, None] // 64 == p[None, :] // 64)
    rt = np.zeros((128, 128), np.float32)
    for m in range(128):
        if m % 64 < 32:
            rt[m + 32, m] = -1.0
        else:
            rt[m - 32, m] = 1.0
    c[:, C_ROT:C_ROT + 128] = rt
    same = (p[:, None] // 64 == p[None, :] // 64)
    c[:, C_TRI:C_TRI + 128] = (p[:, None] <= p[None, :])
    c[:, C_TREV:C_TREV + 128] = same & (p[:, None] > p[None, :])
    q = np.arange(64)
    c[:, C_GM:C_GM + 64] = ((p[:, None] % 64) <= q[None, :])
    s64 = np.arange(64)
    sm = (s64[:, None] // 4 == s64[None, :] // 4)
    c[:64, C_TRIS:C_TRIS + 64] = sm & (s64[:, None] <= s64[None, :])
    c[:64, C_TREVS:C_TREVS + 64] = sm & (s64[:, None] > s64[None, :])
    c[:64, C_GMS:C_GMS + 64] = sm & (s64[:, None] <= s64[None, :])
    own = (p[:, None] <= p[None, :]).astype(np.float32)
    prev = (p[:, None] > p[None, :]).astype(np.float32)
    c[:, C_OWN:C_OWN + 512] = np.tile(own, (1, 4))
    c[:, C_PREV:C_PREV + 512] = np.tile(prev, (1, 4))
    tq = (s64 % 4)
    sc = (p[:, None] >= (tq[None, :] + 1)).astype(np.float32)
    c[:, C_SC:C_SC + 256] = np.tile(sc, (1, 4))
    sn = (sm & (s64[:, None] <= s64[None, :])).astype(np.float32)
    c[:64, C_SN:C_SN + 256] = np.tile(sn, (1, 4))
    c[:64, C_BM:C_BM + 16] = (s64[:, None] // 4 == np.arange(16)[None, :])
    return c


def _rope_tables():
    half = 32
    inv = (10000.0 ** (-np.arange(half, dtype=np.float32) / half)).astype(np.float32)
    pos = np.concatenate([np.arange(SEQ), np.tile(16384 + np.arange(4), NSEQ_S)]).astype(np.float32)
    ang = pos[None, :] * inv[(np.arange(128) % 64) % 32][:, None]
    ang = ang.astype(np.float32)
    return np.stack([np.cos(ang), np.sin(ang)]).astype(np.float32)


def _layer_params(inp, l):
    a = np.zeros((128, NPL), np.float32)
    for i, nm in enumerate(["ffn1_norm", "mix_norm", "xa_norm", "ffn2_norm", "mem_norm"]):
        a[:, PL_G + i * 8:PL_G + (i + 1) * 8] = inp[nm][l].reshape(8, 128).T
    p = np.arange(128)
    a[:, PL_QN] = inp["swa_q_norm"][l][p % 64]
    a[:, PL_KN] = inp["swa_k_norm"][l][p % 64]
    a[:, PL_ON] = inp["gla_out_norm"][l]
    a[:, PL_XQ] = inp["xa_q_norm"][l]
    a[:, PL_SINK:PL_SINK + 4] = inp["swa_sinks"][l].reshape(2, 4)[p // 64]
    a[:, PL_XK:PL_XK + 128] = inp["xa_k_norm"][l][None, :]
    a[0:16, PL_WG:PL_WG + 256] = inp["gla_w_gate"][l]
    a[16, PL_WG:PL_WG + 256] = inp["gla_b_gate"][l]
    return a


def build(n_st=4, stage=99, sts=None):
    nc = bass.Bass("TRN2", target_bir_lowering=False)

    def din(name, shape):
        return nc.dram_tensor(name, list(shape), F32, kind="ExternalInput").ap()

    def dout(name, shape):
        return nc.dram_tensor(name, list(shape), F32, kind="ExternalOutput").ap()

    _specs = {
        "xp": [SEQ, D], "xs": [NS_TOK, D], "ws": [2, NCH, 128, 1024], "wpre": [2, 8, 128, 1024],
        "consts": [128, NCONST], "lpar": [2, 128, NPL], "rope": [2, 128, SEQ + NS_TOK], "mem": [256, D],
        "c_swak": [2, NSEQ_S, 128, 128], "c_swav": [2, NSEQ_S, 128, 128], "c_gla": [2, NSEQ_S, 4, 64, 128],
        "c_memk": [2, NSEQ_S, 256, 512], "c_memv": [2, NSEQ_S, 256, 512],
    }
    _ospecs = {
        "yp": [SEQ, D], "ys": [NS_TOK, D], "o_swak": [2, 128, 128], "o_swav": [2, 128, 128],
        "o_gla": [2, 4, 64, 128], "o_memk": [2, 256, 512], "o_memv": [2, 256, 512],
        "o_swaks": [2, NSEQ_S, 128, 128], "o_swavs": [2, NSEQ_S, 128, 128], "o_glas": [2, NSEQ_S, 4, 64, 128],
    }
    _decl = {}

    def T(name):
        if name not in _decl:
            if name in _specs:
                _decl[name] = din(name, _specs[name])
            else:
                _decl[name] = dout(name, _ospecs[name])
        return _decl[name]

    P = Prog(nc)
    TMAX = TS + NS_TOK
    x = P.sbuf("x", [128, 8, TMAX], F32)
    xn = P.sbuf("xn", [128, 8, TMAX], BF16)
    hb = P.sbuf("hb", [128, NJ, TMAX], BF16)
    slots = [P.sbuf("wsl%d" % i, [128, 1024], BF16) for i in range(NSLOT)]
    sg = [P.sbuf("sg%d" % i, [128, 512], F32) for i in range(4)]
    sq = [P.sbuf("sq%d" % i, [128, 512], BF16) for i in range(2)]
    rstd = P.sbuf("rstd", [128, 512], F32)
    rstd2 = P.sbuf("rstd2", [128, 512], F32)
    chain = {"n": 0}

    def scratch():
        chain["n"] += 1
        if chain["n"] % 2:
            return rstd, "rstd", sg[0], "sg0", sg[1], "sg1"
        return rstd2, "rstd2", sg[2], "sg2", sg[3], "sg3"

    stg = [P.sbuf("stg%d" % i, [128, 1024], F32) for i in range(2)]
    cst = P.sbuf("cst", [128, NCONST], F32)
    lp = P.sbuf("lp", [128, 2, NPL], F32)
    onesb = P.sbuf("onesb", [128, 128], BF16)
    B = [P.psum("B%d" % i, [128, 512]) for i in range(8)]

    def MM(out, lhsT, rhs, start, stop, r, w):
        P.op("pe", lambda e: e.matmul(out, lhsT=lhsT, rhs=rhs, start=start, stop=stop), reads=r, writes=w)

    def TR(out, in_, r, w):
        P.op("pe", lambda e: e.transpose(out=out, in_=in_, identity=cst[0:in_.shape[0], C_ID:C_ID + in_.shape[0]]),
             reads=list(r) + ["cst"], writes=w)

    def ACT(out, in_, func, r, w, **kw):
        P.op("act", lambda e: e.activation(out=out, in_=in_, func=func, **kw), reads=r, writes=w)

    def STT(out, in0, scalar, in1, op0, op1, r, w, eng="dve"):
        P.op(eng, lambda e: e.scalar_tensor_tensor(out=out, in0=in0, scalar=scalar, in1=in1, op0=op0, op1=op1),
             reads=r, writes=w)

    def TT(out, in0, in1, op, r, w, eng="dve"):
        P.op(eng, lambda e: e.tensor_tensor(out=out, in0=in0, in1=in1, op=op), reads=r, writes=w)

    def RECIP(out, in_, r, w):
        P.op("dve", lambda e: e.reciprocal(out=out, in_=in_), reads=r, writes=w)

    def COPY(out, in_, r, w, eng="dve"):
        P.op(eng, lambda e: e.tensor_copy(out=out, in_=in_), reads=r, writes=w)

    def DMA(out, in_, r, w, key=None, eng="sp"):
        P.dma(eng, lambda e: e.dma_start(out=out, in_=in_), reads=r, writes=w, key=key)

    DMA(cst[:], T("consts"), [], ["cst"])
    DMA(lp[:, 0, :], T("lpar")[0], [], ["lp"])
    DMA(lp[:, 1, :], T("lpar")[1], ["lp"], ["lp"])
    COPY(onesb[:], cst[:, C_ONES:C_ONES + 128], ["cst"], ["onesb"])


    TT_ = TMAX
    kTb = P.sbuf("kTb", [128, 2, 128 + TMAX], BF16)
    Vb = P.sbuf("Vb", [128, 2, 10, 128], BF16)
    Sst = P.sbuf("Sst", [128, 2, 2, 128], F32)
    Sbf = P.sbuf("Sbf", [128, 2, 2, 128], BF16)
    kTm = P.sbuf("kTm", [128, 2, 4, 256], BF16)
    Vm = P.sbuf("Vm", [128, 2, 2, 512], BF16)
    rp = P.sbuf("rp", [128, 2, TMAX], F32)
    rot = P.sbuf("rot", [128, 2, 2, 128], F32)
    esk = P.sbuf("esk", [128, 2, 4], F32)
    bd64b = P.sbuf("bd64b", [128, 128], BF16)
    mown = P.sbuf("mown", [128, 512], BF16)
    mprev = P.sbuf("mprev", [128, 512], BF16)
    msc = P.sbuf("msc", [128, 256], BF16)
    msn = P.sbuf("msn", [128, 256], BF16)
    mgs = P.sbuf("mgs", [128, 64], BF16)
    wgb = P.sbuf("wgb", [32, 2, 256], BF16)
    lrT = P.sbuf("lrT", [32, TMAX], BF16)
    pbuf = [P.sbuf("pb%d" % i, [128, 512], BF16) for i in range(2)]
    dn = P.sbuf("dn", [128, 512], F32)
    Asb = P.sbuf("Asb", [128, 4, 128], BF16)
    lsp = P.sbuf("lsp", [128, 256], F32)
    nbs = P.sbuf("nbs", [128, 2, NS_TOK], F32)
    ebl = P.sbuf("ebl", [128, 2, 9 + 16], F32)
    ko = P.sbuf("ko", [128, 128], F32)
    vo = P.sbuf("vo", [128, 128], F32)
    kTs = P.sbuf("kTs", [128, 4, 256], BF16)
    Vs = P.sbuf("Vs", [128, 2, 512], BF16)
    psm = P.sbuf("psm", [128, 32], BF16)
    ss4 = P.sbuf("ss4", [128, 8], F32)
    pnew = P.sbuf("pnew", [64, 2, 256], BF16)
    pcach = P.sbuf("pcach", [128, 2, 256], BF16)
    Kbl = P.sbuf("Kbl", [64, 4, 256], BF16)
    bmb = P.sbuf("bmb", [64, 16], BF16)
    zb = P.sbuf("zb", [128, 128], BF16)
    epst = P.sbuf("epst", [128, 1], F32)
    qxs = P.sbuf("qxs", [128, 4, NS_TOK], BF16)
    qz = P.sbuf("qz", [128, 2, 2, NS_TOK], BF16)
    msx = P.sbuf("msx", [128, NSEQ_S, 64], BF16)
    hbf = hb[:].rearrange("p j t -> p (j t)")
    _cv = {"o": 0}

    def carve(shape, dtype):
        n = int(np.prod(shape[1:]))
        nb = n if dtype == BF16 else 2 * n
        a = _cv["o"]
        _cv["o"] += nb
        assert _cv["o"] <= NJ * TMAX, _cv["o"]
        v = hbf[:, a:a + nb]
        if dtype != BF16:
            v = v.bitcast(F32)
        if len(shape) == 3:
            v = v.rearrange("p (a b) -> p a b", a=shape[1])
        return v

    qT = carve([128, 4, TMAX], BF16)
    qg = carve([128, 2, TMAX], BF16)
    kg = carve([128, 2, TMAX], BF16)
    ktok = carve([128, 9, 256], BF16)
    Vg = carve([128, 9, 512], BF16)
    gate = carve([128, 4, TMAX], BF16)
    S0 = carve([128, 2 * 4, 128], F32)
    S0b = carve([128, 2 * 4, 128], BF16)
    Kst = P.sbuf("Kst", [128, 4, 128], F32)
    KcT = P.sbuf("KcT", [128, 4, 128], BF16)
    Vc = P.sbuf("Vc", [128, 4, 128], BF16)

    def fence(key="HBF"):
        P.op("dve", lambda e: e.memset(ebl[0:1, 0, 8:9], 0.0), reads=[], writes=[key])

    for (dst, c0_, n_) in ((bd64b, C_BD64, 128), (mown, C_OWN, 512), (mprev, C_PREV, 512), (msc, C_SC, 256),
                           (msn, C_SN, 256), (mgs, C_GMS, 64)):
        COPY(dst[0:128, :], cst[:, c0_:c0_ + n_], ["cst"], ["cm%d" % c0_])
    for l in range(2):
        COPY(wgb[:, l, :], lp[0:32, l, PL_WG:PL_WG + 256], ["lp"], ["wgb"])
        for qk, col in ((0, PL_QN), (1, PL_KN)):
            P.op("dve", lambda e, l=l, qk=qk, col=col: e.tensor_scalar(
                out=rot[:, l, qk, :], in0=cst[:, C_ROT:C_ROT + 128], scalar1=lp[:, l, col:col + 1], scalar2=None,
                op0=ALU.mult), reads=["cst", "lp"], writes=["rot"])
        ACT(esk[:, l, :], lp[:, l, PL_SINK:PL_SINK + 4], AF.Exp, ["lp"], ["esk"])
    COPY(bmb[:], cst[0:64, C_BM:C_BM + 16], ["cst"], ["bmb"])
    P.op("dve", lambda e: e.memset(zb[:], 0.0), writes=["zb"])
    P.op("dve", lambda e: e.memset(epst[:], EPS), writes=["epst"])
    P.op("dve", lambda e: e.memset(msx[:], 0.0), writes=["msx"])
    for s_ in range(NSEQ_S):
        P.op("dve", lambda e, s_=s_: e.memset(msx[:, s_, 4 * s_:4 * s_ + 4], 1.0), reads=["msx"], writes=["msx"])
    P.op("dve", lambda e: e.memset(Sst[:], 0.0), writes=["Sst0", "Sst1"])
    P.op("dve", lambda e: e.memset(Sbf[:], 0.0), writes=["Sbf0", "Sbf1"])
    P.op("dve", lambda e: e.memset(lrT[:], 1.0), writes=["lrT"])
    P.op("dve", lambda e: e.memset(Vb[:], 0.0), writes=["Vb0", "Vb1"])
    P.op("dve", lambda e: e.memset(kTb[:], 0.0), writes=["kTb0", "kTb1"])

    wstate = {"n": 0}

    def next_w(l, idx):
        s = wstate["n"] % NSLOT
        wstate["n"] += 1
        key = "W%d" % s
        P.dma("pool", lambda e: e.dma_start(out=slots[s][:], in_=T("ws")[l, idx]), reads=[], writes=[key])
        return slots[s], key

    def gain(l, i, c):
        return lp[:, l, PL_G + i * 8 + c:PL_G + i * 8 + c + 1]

    sqi = {"n": 0}

    def rmsnorm(l, gi, c0, n, si, bank=7):
        bk = "B%d" % bank
        for c in range(8):
            k = sqi["n"] % 2
            sqi["n"] += 1
            ACT(sq[k][:, 0:n], x[:, c, c0:c0 + n], AF.Square, ["x%d_%d" % (c, si)], ["sq%d" % k])
            MM(B[bank][:, 0:n], onesb[:], sq[k][:, 0:n], c == 0, c == 7, ["sq%d" % k, "onesb"], [bk])
        rt, rk = scratch()[0:2]
        ACT(rt[:, 0:n], B[bank][:, 0:n], AF.Ln, [bk], [rk], scale=1.0 / D, bias=epst[:, 0:1])
        ACT(rt[:, 0:n], rt[:, 0:n], AF.Exp, [rk], [rk], scale=-0.5)
        for c in range(8):
            STT(xn[:, c, c0:c0 + n], x[:, c, c0:c0 + n], gain(l, gi, c), rt[:, 0:n], ALU.mult, ALU.mult,
                ["x%d_%d" % (c, si), rk, "lp"], ["xn%d_%d" % (c, si)])

    def ffn(l, which, subs):
        base = 0 if which == 1 else 103
        gi = 0 if which == 1 else 3
        fence()
        for si, (c0, n) in enumerate(subs):
            rmsnorm(l, gi, c0, n, si)
        it = 0
        for j in range(NJ):
            wgt, wgk = next_w(l, base + 2 * j)
            wut, wuk = next_w(l, base + 2 * j + 1)
            for si, (c0, n) in enumerate(subs):
                pb = (it % 2) * 2
                it += 1
                for c in range(8):
                    MM(B[pb][:, 0:n], wgt[:, c * 128:(c + 1) * 128], xn[:, c, c0:c0 + n], c == 0, c == 7,
                       [wgk, "xn%d_%d" % (c, si)], ["B%d" % pb])
                for c in range(8):
                    MM(B[pb + 1][:, 0:n], wut[:, c * 128:(c + 1) * 128], xn[:, c, c0:c0 + n], c == 0, c == 7,
                       [wuk, "xn%d_%d" % (c, si)], ["B%d" % (pb + 1)])
                k = it % 2
                ACT(sg[k][:, 0:n], B[pb][:, 0:n], AF.Silu, ["B%d" % pb], ["sg%d" % k])
                TT(hb[:, j, c0:c0 + n], sg[k][:, 0:n], B[pb + 1][:, 0:n], ALU.mult,
                   ["sg%d" % k, "B%d" % (pb + 1), "HBF"], ["h%d_%d" % (j, si)])
        for dc in range(8):
            for pc in range(3):
                wt, wk = next_w(l, base + 44 + dc * 3 + pc)
                nj = 8 if pc < 2 else 6
                for si, (c0, n) in enumerate(subs):
                    for jj in range(nj):
                        j = pc * 8 + jj
                        MM(B[4 + si][:, 0:n], wt[:, jj * 128:(jj + 1) * 128], hb[:, j, c0:c0 + n],
                           j == 0, j == NJ - 1, [wk, "h%d_%d" % (j, si), "HBF"], ["B%d" % (4 + si)])
            for si, (c0, n) in enumerate(subs):
                STT(x[:, dc, c0:c0 + n], B[4 + si][:, 0:n], 0.5, x[:, dc, c0:c0 + n], ALU.mult, ALU.add,
                    ["B%d" % (4 + si), "x%d_%d" % (dc, si)], ["x%d_%d" % (dc, si)])


    def negb(hc, c0, n):
        if c0 < TS:
            return stg[hc][:, c0:c0 + n], ["stg%d" % hc]
        return nbs[:, hc, 0:n], ["nbs"]

    fmb = {"n": 0}

    def fm_bank():
        fmb["n"] += 1
        return 0 if fmb["n"] % 2 else 4

    def proj_fm(wt, wk, si, c0, n, bank, m=128):
        for c in range(8):
            MM(B[bank][0:m, 0:n], wt[:, c * 128:c * 128 + m], xn[:, c, c0:c0 + n], c == 0, c == 7,
               [wk, "xn%d_%d" % (c, si)], ["B%d" % bank])

    tmb = {"n": 0}

    def tm_bank():
        tmb["n"] += 1
        return 1 if tmb["n"] % 2 else 3

    def proj_tm(wt, wk, si, col0, rows, bank, boff):
        for c in range(8):
            MM(B[bank][0:rows, boff:boff + 128], xn[:, c, col0:col0 + rows], wt[:, c * 128:(c + 1) * 128], c == 0, c == 7,
               [wk, "xn%d_%d" % (c, si)], ["B%d" % bank])

    def qknorm_rope(l, qk, pb, c0, n, out_bf, out_keys):
        bk = "B%d" % pb
        gcol = PL_QN if qk == 0 else PL_KN
        rt, rk, sa, ka, sb, kb = scratch()
        ACT(sa[:, 0:n], B[pb][:, 0:n], AF.Copy, [bk], [ka])
        k = sqi["n"] % 2
        sqi["n"] += 1
        ACT(sq[k][:, 0:n], B[pb][:, 0:n], AF.Square, [bk], ["sq%d" % k])
        MM(B[7][:, 0:n], bd64b[:], sq[k][:, 0:n], True, True, ["sq%d" % k, "cm%d" % C_BD64], ["B7"])
        MM(B[6][:, 0:n], rot[:, l, qk, :], sa[:, 0:n], True, True, [ka, "rot"], ["B6"])
        ACT(rt[:, 0:n], B[7][:, 0:n], AF.Ln, ["B7"], [rk], scale=1.0 / 64, bias=epst[:, 0:1])
        ACT(rt[:, 0:n], rt[:, 0:n], AF.Exp, [rk], [rk], scale=-0.5)
        STT(sb[:, 0:n], sa[:, 0:n], lp[:, l, gcol:gcol + 1], rp[:, 0, c0:c0 + n], ALU.mult, ALU.mult,
            [ka, "lp", "rp"], [kb])
        TT(sa[:, 0:n], B[6][:, 0:n], rp[:, 1, c0:c0 + n], ALU.mult, ["B6", "rp", ka], [ka])
        TT(sb[:, 0:n], sb[:, 0:n], sa[:, 0:n], ALU.add, [ka, kb], [kb])
        TT(sb[:, 0:n], sb[:, 0:n], rt[:, 0:n], ALU.mult, [kb, rk], [kb])
        ACT(out_bf, sb[:, 0:n], AF.Copy, [kb, "HBF"], out_keys)
        return sb, kb

    def swa_block(l, st, blk, part):
        q0 = blk * 128
        si = blk // 4
        first = (st == 0 and blk == 0)
        kbs = ([] if first else [(0, blk, mprev)]) + [(1, blk + 1, mown)]
        if part[0] == "s":
            g = part[1]
            gp = slice(g * 64, (g + 1) * 64)
            for ii, (own, vb, mask) in enumerate(kbs):
                bank = own
                for h in range(4):
                    MM(B[bank][:, h * 128:(h + 1) * 128], kTb[gp, l, vb * 128:(vb + 1) * 128], qT[gp, h, q0:q0 + 128],
                       True, True, ["kTb%d" % l, "qT%d_%d" % (h, si), "HBF"], ["B%d" % bank])
                ACT(pbuf[own][:], B[bank][:], AF.Exp, ["B%d" % bank], ["pb%d" % own], scale=0.125)
                TT(pbuf[own][:], pbuf[own][:], mask[:], ALU.mult, ["pb%d" % own, "cm%d" % (C_OWN if own else C_PREV)],
                   ["pb%d" % own])
        elif part[0] == "p":
            g = part[1]
            gp = slice(g * 64, (g + 1) * 64)
            for ii, (own, vb, mask) in enumerate(kbs):
                MM(B[2][gp, :], Vb[:, l, vb, g * 64:(g + 1) * 64], pbuf[own][:], ii == 0, ii == len(kbs) - 1,
                   ["Vb%d" % l, "pb%d" % own], ["B2"])
            for ii, (own, vb, mask) in enumerate(kbs):
                MM(B[3][gp, :], onesb[:, 0:64], pbuf[own][:], ii == 0, ii == len(kbs) - 1,
                   ["onesb", "pb%d" % own], ["B3"])
        else:
            dnv = dn[:].rearrange("p (h q) -> p h q", h=4)
            TT(dnv, B[3][:].rearrange("p (h q) -> p h q", h=4), esk[:, l, :].unsqueeze(2).to_broadcast([128, 4, 128]), ALU.add,
               ["B3", "esk"], ["dn"])
            ACT(dn[:], dn[:], AF.Ln, ["dn"], ["dn"])
            ACT(dn[:], dn[:], AF.Exp, ["dn"], ["dn"], scale=-1.0)
            TT(xn[:, 0:4, q0:q0 + 128], B[2][:].rearrange("p (h q) -> p h q", h=4), dnv, ALU.mult,
               ["B2", "dn"], ["xn%d_%d" % (m, si) for m in range(4)])

    def gla_block(l, bi, col0, rows, si, maskt, maskkey, sample=False, part=None):
        cs = slice(col0, col0 + rows)
        for h in range(4):
            if part not in (None, "A"):
                break
            hc, hp = h // 2, (h % 2) * 64
            bank = 6 + (h % 2)
            MM(B[bank][0:rows, hc * 128:hc * 128 + rows], kg[hp:hp + 64, hc, cs], qg[hp:hp + 64, hc, cs], True, True,
               ["kg%d_%d" % (hc, si), "qg%d_%d" % (hc, si), "HBF"], ["B%d" % bank])
        for par in range(2):
            if part not in (None, "A"):
                break
            bank = 6 + par
            for hc in range(2):
                h = 2 * hc + par
                TT(Asb[0:rows, h, 0:rows], B[bank][0:rows, hc * 128:hc * 128 + rows], maskt[0:rows, 0:rows], ALU.mult,
                   ["B%d" % bank, maskkey], ["Asb"])
        if not sample:
            for h in range(4):
                if part not in (None, "IO"):
                    break
                hc, hp = h // 2, (h % 2) * 64
                MM(B[4][:, h * 128:h * 128 + rows], Sbf[hp:hp + 64, l, hc, :], qg[hp:hp + 64, hc, cs], True, False,
                   ["Sbf%d" % l, "qg%d_%d" % (hc, si), "HBF"], ["B4"])
                MM(B[4][:, h * 128:h * 128 + rows], Vg[0:128, bi, h * 128:(h + 1) * 128], Asb[:, h, 0:rows], False, True,
                   ["Vg%d" % bi, "Asb", "HBF"], ["B4"])
            if part not in (None, "S"):
                return
            for h in range(4):
                hc, hp = h // 2, (h % 2) * 64
                MM(B[5][hp:hp + 64, hc * 128:(hc + 1) * 128], ktok[0:128, bi, h * 64:(h + 1) * 64],
                   Vg[0:128, bi, h * 128:(h + 1) * 128], True, True, ["ktok%d" % bi, "Vg%d" % bi, "HBF"], ["B5"])
            for hc in range(2):
                TT(dn[:, hc * 128:(hc + 1) * 128], B[5][:, hc * 128:(hc + 1) * 128], Sst[:, l, hc, :], ALU.add,
                   ["B5", "Sst%d" % l], ["dn"])
                P.op("dve", lambda e, hc=hc: e.tensor_scalar(out=Sst[:, l, hc, :], in0=dn[:, hc * 128:(hc + 1) * 128],
                                                            scalar1=ebl[:, hc, bi:bi + 1], scalar2=None, op0=ALU.mult),
                     reads=["dn", "ebl"], writes=["Sst%d" % l])
            ACT(Sbf[:, l, :, :], Sst[:, l, :, :], AF.Copy, ["Sst%d" % l], ["Sbf%d" % l])

    def gla_out(l, col0, rows, si):
        n = 512
        k = sqi["n"] % 2
        sqi["n"] += 1
        ACT(sq[k][:], B[4][:], AF.Square, ["B4"], ["sq%d" % k])
        MM(B[5][:], onesb[:], sq[k][:], True, True, ["sq%d" % k, "onesb"], ["B5"])
        rt, rk, sa, ka, sb, kb = scratch()
        ACT(rt[:], B[5][:], AF.Ln, ["B5"], [rk], scale=1.0 / 128, bias=epst[:, 0:1])
        ACT(rt[:], rt[:], AF.Exp, [rk], [rk], scale=-0.5)
        STT(sa[:], B[4][:], lp[:, l, PL_ON:PL_ON + 1], rt[:], ALU.mult, ALU.mult, ["B4", "lp", rk], [ka])
        TT(xn[:, 4:8, col0:col0 + rows], sa[:].rearrange("p (h q) -> p h q", h=4)[:, :, 0:rows],
           gate[:, :, col0:col0 + rows], ALU.mult, [ka, "HBF"] + ["gate%d_%d" % (h, si) for h in range(4)],
           ["xn%d_%d" % (m, si) for m in range(4, 8)])

    def mixer(l, st, subs, last):
        base = 68
        fence()
        for si, (c0, n) in enumerate(subs):
            rmsnorm(l, 1, c0, n, si)
        nblk = 8
        wt, wk = next_w(l, base + 0)
        for si, (c0, n) in enumerate(subs):
            fb = fm_bank()
            proj_fm(wt, wk, si, c0, n, fb)
            kres, kresk = qknorm_rope(l, 1, fb, c0, n, kTb[:, l, 128 + c0:128 + c0 + n], ["kTb%d" % l])
            if last and si >= 1:
                off = 384 if si == 1 else 0
                rows = 128 if si == 1 else NS_TOK
                TR(B[7][0:rows, 0:128], kres[:, off:off + rows], [kresk], ["B7"])
                ACT(ko[0:rows, :], B[7][0:rows, 0:128], AF.Copy, ["B7"], ["ko"])
                if si == 1:
                    DMA(T("o_swak")[l], ko[:, :], ["ko"], ["out_ko"], key="out_ko")
                else:
                    for sq_ in range(NSEQ_S):
                        DMA(T("o_swaks")[l, sq_, 124:128, :], ko[4 * sq_:4 * sq_ + 4, :], ["ko"], ["out_ko"], key="out_ko")
        wt, wk = next_w(l, base + 1)
        for si, (c0, n) in enumerate(subs):
            rows = min(n, 128)
            nb = n // rows
            tb = tm_bank()
            for b_ in range(nb):
                proj_tm(wt, wk, si, c0 + b_ * rows, rows, tb, b_ * 128)
            blk0 = 1 + c0 // 128
            ACT(Vb[0:rows, l, blk0:blk0 + nb, :], B[tb][0:rows, 0:nb * 128].rearrange("p (b f) -> p b f", b=nb), AF.Copy,
                ["B%d" % tb], ["Vb%d" % l])
            if last and si >= 1:
                off = 384 if si == 1 else 0
                COPY(vo[0:rows, :], B[tb][0:rows, off:off + 128], ["B%d" % tb], ["vo"])
                if si == 1:
                    DMA(T("o_swav")[l], vo[:, :], ["vo"], ["out_vo"], key="out_vo")
                else:
                    for sq_ in range(NSEQ_S):
                        DMA(T("o_swavs")[l, sq_, 124:128, :], vo[4 * sq_:4 * sq_ + 4, :], ["vo"], ["out_vo"], key="out_vo")
        for c_ in range(4):
            wt, wk = next_w(l, base + 2 + c_)
            for si, (c0, n) in enumerate(subs):
                fb = fm_bank()
                proj_fm(wt, wk, si, c0, n, fb)
                qknorm_rope(l, 0, fb, c0, n, qT[:, c_, c0:c0 + n], ["qT%d_%d" % (c_, si)])
        wt, wk = next_w(l, base + 6)
        for si, (c0, n) in enumerate(subs):
            proj_fm(wt, wk, si, c0, n, 0, m=16)
            ACT(lrT[0:16, c0:c0 + n], B[0][0:16, 0:n], AF.Copy, ["B0"], ["lrT%d" % si])
            rows = min(n, 128)
            for b_ in range(n // rows):
                col0 = c0 + b_ * rows
                MM(B[1][0:rows, 0:256], lrT[0:32, col0:col0 + rows], wgb[:, l, :], True, True, ["lrT%d" % si, "wgb"], ["B1"])
                ACT(lsp[0:rows, :], B[1][0:rows, 0:256], AF.Exp, ["B1"], ["lsp"], scale=-1.0)
                ACT(lsp[0:rows, :], lsp[0:rows, :], AF.Ln, ["lsp"], ["lsp"], bias=1.0)
                tri = cst[0:rows, C_TRI:C_TRI + rows] if rows == 128 else cst[0:rows, C_TRIS:C_TRIS + rows]
                for hc in range(2):
                    MM(B[2][:, hc * 128:hc * 128 + rows], lsp[0:rows, hc * 128:(hc + 1) * 128], tri, True, True,
                       ["lsp", "cst"], ["B2"])
                    dst, dk = negb(hc, col0, rows)
                    ACT(dst, B[2][:, hc * 128:hc * 128 + rows], AF.Copy, ["B2"], dk + ["nb%d_%d" % (hc, si)])
            for hc in range(2):
                src, sk = negb(hc, c0, n)
                if n >= 128:
                    nb = n // 128
                    ACT(ebl[:, hc, (c0 // 128):(c0 // 128) + nb], src[:, 127:n:128], AF.Exp, sk + ["nb%d_%d" % (hc, si)], ["ebl"],
                        scale=-1.0 / 16)
                else:
                    ACT(ebl[:, hc, 9:9 + NSEQ_S], src[:, 3:n:4], AF.Exp, sk + ["nb%d_%d" % (hc, si)], ["ebl"], scale=-1.0 / 16)
        for hc in range(2):
            wt, wk = next_w(l, base + 7 + hc)
            for si, (c0, n) in enumerate(subs):
                fb = fm_bank()
                proj_fm(wt, wk, si, c0, n, fb)
                src, sk = negb(hc, c0, n)
                ACT(sg[0][:, 0:n], src, AF.Exp, sk + ["nb%d_%d" % (hc, si)], ["sg0"], scale=-1.0 / 16)
                STT(qg[:, hc, c0:c0 + n], B[fb][:, 0:n], 0.125, sg[0][:, 0:n], ALU.mult, ALU.mult, ["B%d" % fb, "sg0", "HBF"],
                    ["qg%d_%d" % (hc, si)])
        for hc in range(2):
            wt, wk = next_w(l, base + 9 + hc)
            for si, (c0, n) in enumerate(subs):
                proj_fm(wt, wk, si, c0, n, 0)
                src, sk = negb(hc, c0, n)
                ACT(sg[0][:, 0:n], src, AF.Exp, sk + ["nb%d_%d" % (hc, si)], ["sg0"], scale=1.0 / 16)
                TT(sg[1][:, 0:n], B[0][:, 0:n], sg[0][:, 0:n], ALU.mult, ["B0", "sg0"], ["sg1"])
                ACT(kg[:, hc, c0:c0 + n], sg[1][:, 0:n], AF.Copy, ["sg1", "HBF"], ["kg%d_%d" % (hc, si)])
                rows = min(n, 128)
                nb = n // rows
                for b_ in range(nb):
                    TR(B[1][0:rows, b_ * 128:(b_ + 1) * 128], sg[1][:, b_ * rows:(b_ + 1) * rows], ["sg1"], ["B1"])
                bi0 = c0 // 128
                ACT(ktok[0:rows, bi0:bi0 + nb, hc * 128:(hc + 1) * 128],
                    B[1][0:rows, 0:nb * 128].rearrange("p (b f) -> p b f", b=nb), AF.Copy, ["B1", "HBF"],
                    ["ktok%d" % (bi0 + b_) for b_ in range(nb)])
        for h in range(4):
            wt, wk = next_w(l, base + 11 + h)
            for si, (c0, n) in enumerate(subs):
                rows = min(n, 128)
                nb = n // rows
                tb = tm_bank()
                for b_ in range(nb):
                    proj_tm(wt, wk, si, c0 + b_ * rows, rows, tb, b_ * 128)
                bi0 = c0 // 128
                ACT(Vg[0:rows, bi0:bi0 + nb, h * 128:(h + 1) * 128],
                    B[tb][0:rows, 0:nb * 128].rearrange("p (b f) -> p b f", b=nb), AF.Copy, ["B%d" % tb, "HBF"],
                    ["Vg%d" % (bi0 + b_) for b_ in range(nb)])
        for h in range(4):
            wt, wk = next_w(l, base + 15 + h)
            for si, (c0, n) in enumerate(subs):
                fb = fm_bank()
                proj_fm(wt, wk, si, c0, n, fb)
                ACT(gate[:, h, c0:c0 + n], B[fb][:, 0:n], AF.Silu, ["B%d" % fb, "HBF"], ["gate%d_%d" % (h, si)])
        for blk in range(nblk):
            ga = (l, blk, blk * 128, 128, blk // 4, mown, "cm%d" % C_OWN)
            swa_block(l, st, blk, ("s", 0))
            gla_block(*ga, part="A")
            swa_block(l, st, blk, ("p", 0))
            swa_block(l, st, blk, ("s", 1))
            gla_block(*ga, part="IO")
            swa_block(l, st, blk, ("p", 1))
            gla_block(*ga, part="S")
            swa_block(l, st, blk, ("f",))
            gla_out(l, blk * 128, 128, blk // 4)
        COPY(kTb[:, l, 0:128], kTb[:, l, TS:TS + 128], ["kTb%d" % l], ["kTb%d" % l])
        COPY(Vb[:, l, 0, :], Vb[:, l, 8, :], ["Vb%d" % l], ["Vb%d" % l])
        if last:
            if stage >= 4 or stage == -3:
                sample_mix(l)
            for hc in range(2):
                for hh in range(2):
                    DMA(T("o_gla")[l, 2 * hc + hh], Sst[hh * 64:(hh + 1) * 64, l, hc, :], ["Sst%d" % l], ["out_gla"], key="out_gla")
        for dc in range(8):
            wt, wk = next_w(l, base + 19 + dc)
            for si, (c0, n) in enumerate(subs):
                for m in range(8):
                    MM(B[si][:, 0:n], wt[:, m * 128:(m + 1) * 128], xn[:, m, c0:c0 + n], m == 0, m == 7,
                       [wk, "xn%d_%d" % (m, si)], ["B%d" % si])
                TT(x[:, dc, c0:c0 + n], B[si][:, 0:n], x[:, dc, c0:c0 + n], ALU.add,
                   ["B%d" % si, "x%d_%d" % (dc, si)], ["x%d_%d" % (dc, si)])
        fence()


    def next_w_pre(l, i):
        s_ = wstate["n"] % NSLOT
        wstate["n"] += 1
        key = "W%d" % s_
        P.dma("pool", lambda e: e.dma_start(out=slots[s_][:], in_=T("wpre")[l, i]), reads=[], writes=[key])
        return slots[s_], key

    def memkv():
        for blk in range(2):
            load_x(T("mem")[blk * 128:(blk + 1) * 128, :], 128, blk * 128, lambda c: 0)
        for l in range(2):
            rmsnorm(l, 4, 0, 256, 0)
            for kv in range(2):
                for h in range(4):
                    wt, wk = next_w_pre(l, kv * 4 + h)
                    for mt in range(2):
                        proj_tm(wt, wk, 0, mt * 128, 128, mt, h * 128)
                for mt in range(2):
                    bk = "B%d" % mt
                    if kv == 0:
                        for h in range(4):
                            ACT(sg[0][:, h * 128:(h + 1) * 128], B[mt][:, h * 128:(h + 1) * 128], AF.Square, [bk], ["sg0", "ss4"],
                                accum_out=ss4[:, h:h + 1])
                        ACT(ss4[:, 4:8], ss4[:, 0:4], AF.Sqrt, ["ss4"], ["ss4"], scale=1.0 / 128, bias=EPS)
                        RECIP(ss4[:, 4:8], ss4[:, 4:8], ["ss4"], ["ss4"])
                        TT(sg[0][:].rearrange("p (h f) -> p h f", h=4), B[mt][:].rearrange("p (h f) -> p h f", h=4),
                           ss4[:, 4:8].unsqueeze(2).to_broadcast([128, 4, 128]), ALU.mult, [bk, "ss4"], ["sg0"])
                        TT(sg[1][:].rearrange("p (h f) -> p h f", h=4), sg[0][:].rearrange("p (h f) -> p h f", h=4),
                           lp[:, l, PL_XK:PL_XK + 128].unsqueeze(1).to_broadcast([128, 4, 128]), ALU.mult, ["sg0", "lp"], ["sg1"])
                        DMA(T("o_memk")[l, mt * 128:(mt + 1) * 128, :], sg[1][:], ["sg1"], ["out_mk"], key="out_mk")
                        for h in range(4):
                            TR(B[2][:, h * 128:(h + 1) * 128], sg[1][:, h * 128:(h + 1) * 128], ["sg1"], ["B2"])
                        ACT(kTm[:, l, :, mt * 128:(mt + 1) * 128], B[2][:].rearrange("p (h m) -> p h m", h=4), AF.Copy,
                            ["B2"], ["kTm%d" % l])
                    else:
                        ACT(sg[1][:], B[mt][:], AF.Copy, [bk], ["sg1"])
                        COPY(Vm[:, l, mt, :], B[mt][:], [bk], ["Vm%d" % l])
                        DMA(T("o_memv")[l, mt * 128:(mt + 1) * 128, :], sg[1][:], ["sg1"], ["out_mv"], key="out_mv")

    XS = 128.0 ** -0.5

    def xattn(l, st, subs, last):
        base = 95
        fence()
        for si, (c0, n) in enumerate(subs):
            rmsnorm(l, 2, c0, n, si)
        for h in range(4):
            wt, wk = next_w(l, base + h)
            for si, (c0, n) in enumerate(subs):
                fb = fm_bank()
                proj_fm(wt, wk, si, c0, n, fb)
                k = sqi["n"] % 2
                sqi["n"] += 1
                ACT(sq[k][:, 0:n], B[fb][:, 0:n], AF.Square, ["B%d" % fb], ["sq%d" % k])
                MM(B[7][:, 0:n], onesb[:], sq[k][:, 0:n], True, True, ["sq%d" % k, "onesb"], ["B7"])
                rt, rk = scratch()[0:2]
                ACT(rt[:, 0:n], B[7][:, 0:n], AF.Ln, ["B7"], [rk], scale=1.0 / 128, bias=epst[:, 0:1])
                ACT(rt[:, 0:n], rt[:, 0:n], AF.Exp, [rk], [rk], scale=-0.5)
                if c0 >= TS:
                    STT(qxs[:, h, 0:n], B[fb][:, 0:n], lp[:, l, PL_XQ:PL_XQ + 1], rt[:, 0:n], ALU.mult, ALU.mult,
                        ["B%d" % fb, "lp", rk], ["qxs%d" % h])
                else:
                    STT(qT[:, h, c0:c0 + n], B[fb][:, 0:n], lp[:, l, PL_XQ:PL_XQ + 1], rt[:, 0:n], ALU.mult, ALU.mult,
                        ["B%d" % fb, "lp", rk, "HBF"], ["qT%d_%d" % (h, si)])
        for si, (c0, n) in enumerate(subs[0:2]):
            for h in range(4):
                for mt in range(2):
                    MM(B[mt][:, 0:n], kTm[:, l, h, mt * 128:(mt + 1) * 128], qT[:, h, c0:c0 + n], True, True,
                       ["kTm%d" % l, "qT%d_%d" % (h, si), "HBF"], ["B%d" % mt])
                    ACT(pbuf[mt][:, 0:n], B[mt][:, 0:n], AF.Exp, ["B%d" % mt], ["pb%d" % mt], scale=XS)
                for mt in range(2):
                    MM(B[2][:, 0:n], Vm[:, l, mt, h * 128:(h + 1) * 128], pbuf[mt][:, 0:n], mt == 0, mt == 1,
                       ["Vm%d" % l, "pb%d" % mt], ["B2"])
                for mt in range(2):
                    MM(B[3][:, 0:n], onesb[:], pbuf[mt][:, 0:n], mt == 0, mt == 1, ["onesb", "pb%d" % mt], ["B3"])
                ACT(dn[:, 0:n], B[3][:, 0:n], AF.Ln, ["B3"], ["dn"])
                ACT(dn[:, 0:n], dn[:, 0:n], AF.Exp, ["dn"], ["dn"], scale=-1.0)
                TT(gate[:, h, c0:c0 + n], B[2][:, 0:n], dn[:, 0:n], ALU.mult, ["B2", "dn", "HBF"], ["gate%d_%d" % (h, si)])
        if last and (stage >= 5 or stage in (-2, -3, -4)):
            SC = TS
            P.op("pe", lambda e: e.drain(), reads=[], writes=[])
            MM(B[7][:], zb[:], mown[:], True, False, ["zb", "cm%d" % C_OWN], ["B7"])
            MM(B[3][:], zb[:], mown[:], True, False, ["zb", "cm%d" % C_OWN], ["B3"])
            for sq_ in range(NSEQ_S):
                DMA(stg[0][:].rearrange("p (mt f) -> p mt f", mt=2), T("c_memk")[l, sq_].rearrange("(mt p) f -> p mt f", p=128),
                    [], ["stg0"])
                DMA(stg[1][:].rearrange("p (mt f) -> p mt f", mt=2), T("c_memv")[l, sq_].rearrange("(mt p) f -> p mt f", p=128),
                    [], ["stg1"])
                for mt in range(2):
                    for h in range(4):
                        TR(B[4 + mt][:, h * 128:(h + 1) * 128], stg[0][:, mt * 512 + h * 128:mt * 512 + (h + 1) * 128], ["stg0"],
                           ["B%d" % (4 + mt)])
                    ACT(kTs[:, :, mt * 128:(mt + 1) * 128], B[4 + mt][:].rearrange("p (h m) -> p h m", h=4), AF.Copy,
                        ["B%d" % (4 + mt)], ["kTs"])
                COPY(Vs[:].rearrange("p mt f -> p (mt f)"), stg[1][:], ["stg1"], ["Vs"])
                for mt in range(2):
                    for h in range(4):
                        MM(B[0][:, (mt * 4 + h) * 64:(mt * 4 + h + 1) * 64], kTs[:, h, mt * 128:(mt + 1) * 128], qxs[:, h, :], True, True,
                           ["kTs", "qxs%d" % h], ["B0"])
                ACT(pbuf[0][:], B[0][:], AF.Exp, ["B0"], ["pb0"], scale=XS)
                TT(pbuf[0][:].rearrange("p (a t) -> p a t", a=8), pbuf[0][:].rearrange("p (a t) -> p a t", a=8),
                   msx[:, sq_, :].unsqueeze(1).to_broadcast([128, 8, 64]), ALU.mult, ["pb0", "msx"], ["pb0"])
                for h in range(4):
                    for mt in range(2):
                        MM(B[7][:, h * 64:(h + 1) * 64], Vs[:, mt, h * 128:(h + 1) * 128], pbuf[0][:, (mt * 4 + h) * 64:(mt * 4 + h + 1) * 64],
                           False, (sq_ == NSEQ_S - 1 and h == 3 and mt == 1), ["Vs", "pb0"], ["B7"])
                MM(B[3][:], onesb[:], pbuf[0][:], False, sq_ == NSEQ_S - 1, ["onesb", "pb0"], ["B3"])
            COPY(dn[:, 256:512], B[3][:, 256:512], ["B3"], ["dn"])
            TT(dn[:, 0:256], B[3][:, 0:256], dn[:, 256:512], ALU.add, ["B3", "dn"], ["dn"])
            RECIP(dn[:, 0:256], dn[:, 0:256], ["dn"], ["dn"])
            TT(gate[:, :, SC:SC + 64], B[7][:, 0:256].rearrange("p (h q) -> p h q", h=4), dn[:, 0:256].rearrange("p (h q) -> p h q", h=4),
               ALU.mult, ["B7", "dn", "HBF"], ["gate%d_2" % h for h in range(4)])
        for dcp in range(4):
            wt, wk = next_w(l, base + 4 + dcp)
            for dcc in range(2):
                dc = 2 * dcp + dcc
                for si, (c0, n) in enumerate(subs):
                    for m in range(4):
                        MM(B[si][:, 0:n], wt[:, (dcc * 4 + m) * 128:(dcc * 4 + m + 1) * 128], gate[:, m, c0:c0 + n], m == 0, m == 3,
                           [wk, "gate%d_%d" % (m, si), "HBF"], ["B%d" % si])
                    TT(x[:, dc, c0:c0 + n], B[si][:, 0:n], x[:, dc, c0:c0 + n], ALU.add,
                       ["B%d" % si, "x%d_%d" % (dc, si)], ["x%d_%d" % (dc, si)])
        fence()

    def sample_mix(l):
        SC = TS
        si = 2
        sample_swa(l, SC)
        sample_gla(l, SC)

    def sample_swa(l, SC):
        for g in range(2):
            gp = slice(g * 64, (g + 1) * 64)
            for h in range(4):
                MM(B[g][0:64, h * 64:(h + 1) * 64], kTb[gp, l, 128 + SC:128 + SC + 64], qT[gp, h, SC:SC + 64], True, True,
                   ["kTb%d" % l, "qT%d_2" % h, "HBF"], ["B%d" % g])
            ACT(pnew[:, g, :], B[g][0:64, 0:256], AF.Exp, ["B%d" % g], ["pnew"], scale=0.125)
        TT(pnew[:], pnew[:], msn[0:64, :].unsqueeze(1).to_broadcast([64, 2, 256]), ALU.mult, ["pnew", "cm%d" % C_SN], ["pnew"],
           eng="pool")
        for g in range(2):
            gp = slice(g * 64, (g + 1) * 64)
            MM(B[5][gp, 0:256], Vb[0:64, l, 9, g * 64:(g + 1) * 64], pnew[:, g, :], True, False, ["Vb%d" % l, "pnew"], ["B5"])
            MM(B[6][gp, 0:256], onesb[0:64, 0:64], pnew[:, g, :], True, False, ["onesb", "pnew"], ["B6"])
        DMA(T("o_swaks")[l, :, 0:124, :].rearrange("s k f -> s (k f)"), T("c_swak")[l, :, 4:128, :].rearrange("s k f -> s (k f)"),
            [], ["out_ck"], key="out_ck")
        DMA(T("o_swavs")[l, :, 0:124, :].rearrange("s k f -> s (k f)"), T("c_swav")[l, :, 4:128, :].rearrange("s k f -> s (k f)"),
            [], ["out_ck"], key="out_ck")
        for gi in range(4):
            DMA(Kst[:], T("c_swak")[l, 4 * gi:4 * gi + 4].rearrange("s k f -> k s f"), [], ["Kst"])
            P.dma("pool", lambda e, gi=gi: e.dma_start(out=Vc[:], in_=T("c_swav")[l, 4 * gi:4 * gi + 4].rearrange("s k f -> k s f")),
                  reads=[], writes=["Vc"])
            for j in range(4):
                TR(B[2][:, j * 128:(j + 1) * 128], Kst[:, j, :], ["Kst"], ["B2"])
            ACT(KcT[:], B[2][:].rearrange("p (j k) -> p j k", j=4), AF.Copy, ["B2"], ["KcT"])
            for g in range(2):
                gp = slice(g * 64, (g + 1) * 64)
                for j in range(4):
                    s_ = 4 * gi + j
                    for h in range(4):
                        MM(B[3 + g][:, h * 64 + 4 * s_:h * 64 + 4 * s_ + 4], KcT[gp, j, :], qT[gp, h, SC + 4 * s_:SC + 4 * s_ + 4],
                           True, True, ["KcT", "qT%d_2" % h, "HBF"], ["B%d" % (3 + g)])
                v_in = B[3 + g][:, 0:256].rearrange("p (h t) -> p h t", h=4)[:, :, 16 * gi:16 * gi + 16]
                v_out = pcach[:, g, :].rearrange("p (h t) -> p h t", h=4)[:, :, 16 * gi:16 * gi + 16]
                v_msk = msc[:].rearrange("p (h t) -> p h t", h=4)[:, :, 16 * gi:16 * gi + 16]
                ACT(v_out, v_in, AF.Exp, ["B%d" % (3 + g)], ["pcach"], scale=0.125)
                TT(v_out, v_out, v_msk, ALU.mult, ["pcach", "cm%d" % C_SC], ["pcach"], eng="pool")
            for g in range(2):
                gp = slice(g * 64, (g + 1) * 64)
                for j in range(4):
                    s_ = 4 * gi + j
                    lastmm = (gi == 3 and j == 3)
                    for h in range(4):
                        cc = slice(h * 64 + 4 * s_, h * 64 + 4 * s_ + 4)
                        MM(B[5][gp, cc], Vc[:, j, g * 64:(g + 1) * 64], pcach[:, g, cc], False, lastmm and h == 3, ["Vc", "pcach"], ["B5"])
                        MM(B[6][gp, cc], onesb[:, 0:64], pcach[:, g, cc], False, lastmm and h == 3, ["onesb", "pcach"], ["B6"])
        dnv = dn[:, 0:256].rearrange("p (h q) -> p h q", h=4)
        TT(dnv, B[6][:, 0:256].rearrange("p (h q) -> p h q", h=4), esk[:, l, :].unsqueeze(2).to_broadcast([128, 4, 64]), ALU.add,
           ["B6", "esk"], ["dn"])
        RECIP(dn[:, 0:256], dn[:, 0:256], ["dn"], ["dn"])
        TT(xn[:, 0:4, SC:SC + 64], B[5][:, 0:256].rearrange("p (h q) -> p h q", h=4), dnv, ALU.mult, ["B5", "dn"],
           ["xn%d_2" % m for m in range(4)])
    def sample_gla(l, SC):
        P.op("dve", lambda e: e.memset(Asb[64:128, :, :], 0.0), reads=[], writes=["Asb"])
        P.op("dve", lambda e: e.memset(Vg[64:128, 8, :], 0.0), reads=["HBF"], writes=["Vg8"])
        gla_block(l, 8, SC, 64, 2, mgs, "cm%d" % C_GMS, sample=True)
        P.op("dve", lambda e: e.memset(qz[:], 0.0), reads=[], writes=["qz"])
        COPY(qz[0:64, 0, :, :], qg[0:64, :, SC:SC + 64], ["qg0_2", "qg1_2", "HBF", "qz"], ["qz"])
        COPY(qz[64:128, 1, :, :], qg[64:128, :, SC:SC + 64], ["qg0_2", "qg1_2", "HBF", "qz"], ["qz"])
        MM(B[4][:], zb[:], mown[:], True, False, ["zb", "cm%d" % C_OWN], ["B4"])
        for h in range(4):
            MM(B[4][:, h * 128:h * 128 + 64], Vg[0:128, 8, h * 128:(h + 1) * 128], Asb[:, h, 0:64], False, False,
               ["Vg8", "Asb", "HBF"], ["B4"])
        for gi in range(4):
            for hh in range(2):
                for hc in range(2):
                    DMA(S0[hh * 64:(hh + 1) * 64, hc * 4:(hc + 1) * 4, :],
                        T("c_gla")[l, 4 * gi:4 * gi + 4, 2 * hc + hh].rearrange("s d v -> d s v"), ["HBF"], ["S0"])
            ACT(S0b[:], S0[:], AF.Copy, ["S0", "HBF"], ["S0b"])
            for h in range(4):
                hc, par = h // 2, h % 2
                for j in range(4):
                    s_ = 4 * gi + j
                    MM(B[4][:, h * 128 + 4 * s_:h * 128 + 4 * s_ + 4], S0b[:, hc * 4 + j, :], qz[:, par, hc, 4 * s_:4 * s_ + 4],
                       False, (gi == 3 and j == 3 and h == 3), ["S0b", "qz", "HBF"], ["B4"])
            TT(Kbl[:], ktok[0:64, 8, :].unsqueeze(1).to_broadcast([64, 4, 256]),
               bmb[:, 4 * gi:4 * gi + 4].unsqueeze(2).to_broadcast([64, 4, 256]), ALU.mult, ["ktok8", "bmb", "HBF"], ["Kbl"])
            for h in range(4):
                hc, hp = h // 2, (h % 2) * 64
                for j in range(4):
                    MM(B[5 + hc][hp:hp + 64, j * 128:(j + 1) * 128], Kbl[:, j, h * 64:(h + 1) * 64], Vg[0:64, 8, h * 128:(h + 1) * 128],
                       True, True, ["Kbl", "Vg8", "HBF"], ["B%d" % (5 + hc)])
            for hc in range(2):
                sv = S0[:, hc * 4:(hc + 1) * 4, :]
                TT(sv, B[5 + hc][:].rearrange("p (j v) -> p j v", j=4), sv, ALU.add, ["B%d" % (5 + hc), "S0", "HBF"], ["S0"])
                TT(sv, sv, ebl[:, hc, 9 + 4 * gi:9 + 4 * gi + 4].unsqueeze(2).to_broadcast([128, 4, 128]), ALU.mult,
                   ["S0", "ebl", "HBF"], ["S0"])
            for hh in range(2):
                for hc in range(2):
                    DMA(T("o_glas")[l, 4 * gi:4 * gi + 4, 2 * hc + hh].rearrange("s d v -> d s v"),
                        S0[hh * 64:(hh + 1) * 64, hc * 4:(hc + 1) * 4, :], ["S0", "HBF"], ["out_gs"], key="out_gs")
        gla_out(l, SC, 64, 2)

    def load_x(src, nrows, col0, si_of_col):
        k = load_x.n % 2
        load_x.n += 1
        DMA(stg[k][0:nrows, :], src, [], ["stg%d" % k])
        si = si_of_col(col0)
        for half in range(2):
            bank = 6 + half
            for cc in range(4):
                c = half * 4 + cc
                TR(B[bank][:, cc * 128:cc * 128 + nrows], stg[k][0:nrows, c * 128:(c + 1) * 128], ["stg%d" % k], ["B%d" % bank])
            src_v = B[bank][:].rearrange("p (c n) -> p c n", c=4)[:, :, 0:nrows]
            dst_v = x[:, half * 4:half * 4 + 4, col0:col0 + nrows]
            wk = ["x%d_%d" % (half * 4 + cc, si) for cc in range(4)]
            if half == 0:
                ACT(dst_v, src_v, AF.Copy, ["B%d" % bank], wk)
            else:
                COPY(dst_v, src_v, ["B%d" % bank], wk)
    load_x.n = 0
    load_x.n2 = 0

    def store_x(dst, nrows, col0, si_of_col):
        k = load_x.n % 2
        load_x.n += 1
        si = si_of_col(col0)
        for half in range(2):
            bank = 6 + (load_x.n2 % 2)
            load_x.n2 += 1
            for cc in range(4):
                c = half * 4 + cc
                TR(B[bank][0:nrows, cc * 128:(cc + 1) * 128], x[:, c, col0:col0 + nrows], ["x%d_%d" % (c, si)], ["B%d" % bank])
            if half == 0:
                ACT(stg[k][0:nrows, 0:512], B[bank][0:nrows, :], AF.Copy, ["B%d" % bank], ["stg%d" % k])
            else:
                COPY(stg[k][0:nrows, 512:1024], B[bank][0:nrows, :], ["B%d" % bank], ["stg%d" % k])
        DMA(dst, stg[k][0:nrows, :], ["stg%d" % k], ["out_stg%d" % k], key="out_stg%d" % k)

    if stage >= 3 or stage in (-2, -3, -4):
        memkv()
    for st in (sts if sts is not None else range(n_st)):
        last = (st == 3)
        subs = [(0, 512), (512, 512)] + ([(1024, NS_TOK)] if last else [])

        def si_of_col(col):
            return col // 512

        for blk in range(8):
            load_x(T("xp")[st * TS + blk * 128: st * TS + (blk + 1) * 128, :], 128, blk * 128, si_of_col)
        if last:
            load_x(T("xs")[:, :], NS_TOK, 1024, si_of_col)
        DMA(rp[:, 0, 0:TS], T("rope")[0, :, st * TS:(st + 1) * TS], [], ["rp"])
        DMA(rp[:, 1, 0:TS], T("rope")[1, :, st * TS:(st + 1) * TS], ["rp"], ["rp"])
        if last:
            DMA(rp[:, 0, TS:TMAX], T("rope")[0, :, SEQ:SEQ + NS_TOK], ["rp"], ["rp"])
            DMA(rp[:, 1, TS:TMAX], T("rope")[1, :, SEQ:SEQ + NS_TOK], ["rp"], ["rp"])
        for l in range(2):
            if stage == 0:
                break
            if stage == -2:
                xattn(l, st, subs, last)
                break
            if stage in (-3, -4):
                mixer(l, st, subs, last)
                xattn(l, st, subs, last)
                break
            if stage == -1:
                for si, (c0, n) in enumerate(subs):
                    rmsnorm(l, 0, c0, n, si)
                break
            ffn(l, 1, subs)
            if stage >= 2:
                mixer(l, st, subs, last)
            if stage >= 3:
                xattn(l, st, subs, last)
            if stage >= 5:
                ffn(l, 2, subs)
            if stage < 6:
                break
        for blk in range(8):
            store_x(T("yp")[st * TS + blk * 128: st * TS + (blk + 1) * 128, :], 128, blk * 128, si_of_col)
        if last:
            store_x(T("ys")[:, :], NS_TOK, 1024, si_of_col)

    P.emit()
    return nc, P, _decl


_OSPECS = {
    "yp": [SEQ, D], "ys": [NS_TOK, D], "o_swak": [2, 128, 128], "o_swav": [2, 128, 128],
    "o_gla": [2, 4, 64, 128], "o_memk": [2, 256, 512], "o_memv": [2, 256, 512],
    "o_swaks": [2, NSEQ_S, 128, 128], "o_swavs": [2, NSEQ_S, 128, 128], "o_glas": [2, NSEQ_S, 4, 64, 128],
}


def kernel(n_st=4, stage=99, sts=None, **inp):
    inp = {k: np.asarray(v) for k, v in inp.items()}
    wsarr = np.stack([_layer_stream(inp, l) for l in range(2)])
    wprearr = np.stack([np.stack([_colchunk(inp["xa_wk"][l], range(h * 128, (h + 1) * 128)) for h in range(4)] +
                                 [_colchunk(inp["xa_wv"][l], range(h * 128, (h + 1) * 128)) for h in range(4)])
                        for l in range(2)])
    consts = _consts()
    lpar = np.stack([_layer_params(inp, l) for l in range(2)])
    rope = _rope_tables()
    nc, P, decl = build(n_st=n_st, stage=stage, sts=sts)
    REAL = [0, 1, 4, 5]
    zx = np.zeros((SEQ, D), np.float32)
    zm = np.zeros((256, D), np.float32)
    in_maps = []
    for c in range(8):
        b = REAL.index(c) if c in REAL else None
        s0 = c * NSEQ_S
        in_maps.append({
            "xp": np.ascontiguousarray(inp["x_prompt"][b]) if b is not None else zx,
            "xs": np.ascontiguousarray(inp["x_sample"][s0:s0 + NSEQ_S].reshape(NS_TOK, D)),
            "ws": wsarr, "wpre": wprearr, "consts": consts, "lpar": lpar, "rope": rope,
            "mem": np.ascontiguousarray(inp["mem_prompt"][b]) if b is not None else zm,
            "c_swak": np.ascontiguousarray(inp["cache_swa_k"][:, s0:s0 + NSEQ_S].reshape(2, NSEQ_S, 128, 128)),
            "c_swav": np.ascontiguousarray(inp["cache_swa_v"][:, s0:s0 + NSEQ_S].reshape(2, NSEQ_S, 128, 128)),
            "c_gla": np.ascontiguousarray(inp["state_gla"][:, s0:s0 + NSEQ_S]),
            "c_memk": np.ascontiguousarray(inp["cache_mem_k"][:, s0:s0 + NSEQ_S].reshape(2, NSEQ_S, 256, 512)),
            "c_memv": np.ascontiguousarray(inp["cache_mem_v"][:, s0:s0 + NSEQ_S].reshape(2, NSEQ_S, 256, 512)),
        })
    in_maps = [{k: v for k, v in m.items() if k in decl} for m in in_maps]
    res = run_bass_kernel_spmd(nc, in_maps, core_ids=list(range(8)))
    R = [dict(r) for r in res.results]
    for r in R:
        for k, shp in _OSPECS.items():
            if k not in r:
                r[k] = np.zeros(shp, np.float32)
    y_p = np.stack([R[REAL[b]]["yp"] for b in range(4)])
    y_s = np.concatenate([R[c]["ys"].reshape(NSEQ_S, 4, D) for c in range(8)])
    swak_p = np.stack([R[REAL[b]]["o_swak"] for b in range(4)], axis=1).reshape(2, 4, 128, 2, 64)
    swav_p = np.stack([R[REAL[b]]["o_swav"] for b in range(4)], axis=1).reshape(2, 4, 128, 2, 64)
    gla_p = np.stack([R[REAL[b]]["o_gla"] for b in range(4)], axis=1)
    memk_p = np.stack([R[REAL[b]]["o_memk"] for b in range(4)], axis=1).reshape(2, 4, 256, 4, 128)
    memv_p = np.stack([R[REAL[b]]["o_memv"] for b in range(4)], axis=1).reshape(2, 4, 256, 4, 128)
    swak_s = np.concatenate([R[c]["o_swaks"] for c in range(8)], axis=1).reshape(2, 128, 128, 2, 64)
    swav_s = np.concatenate([R[c]["o_swavs"] for c in range(8)], axis=1).reshape(2, 128, 128, 2, 64)
    gla_s = np.concatenate([R[c]["o_glas"] for c in range(8)], axis=1)
    return (y_p, y_s, swak_p, swav_p, gla_p, memk_p, memv_p, swak_s, swav_s, gla_s)
```
